# Optimizing a Trainium2 kernel written in Bass

```python
import math
import jax, jax.numpy as jnp
from jax import lax
import numpy as np

D_MODEL = 2048
BATCH = 4
SEQ = 4096
DEPTH = 2

CHUNK = 64
N_A = DEPTH // 2
N_B = DEPTH - N_A
GDN_HEADS = 16
GDN_DK = 128
GDN_DV = 128
GDN_CONV = 4
GDN_QK_DIM = GDN_HEADS * GDN_DK
GDN_V_DIM = GDN_HEADS * GDN_DV
GDN_IN_DIM = 2 * GDN_QK_DIM + 2 * GDN_V_DIM + 2 * GDN_HEADS
DIFF_HEADS = 8
DIFF_DH = 128
DIFF_QK_DIM = DIFF_HEADS * 2 * DIFF_DH
DIFF_V_DIM = DIFF_HEADS * 2 * DIFF_DH
Q_BLOCK = 128
D_FF = 5632
FFN_CONV = 3
EPS = 1e-6

kernel_name = 'yoco_gdn_diffattn_convffn_trunk'


def rmsnorm(x, w):
    xf = x.astype(jnp.float32)
    y = xf * lax.rsqrt(jnp.mean(xf * xf, axis=-1, keepdims=True) + EPS)
    return (y * w.astype(jnp.float32)).astype(x.dtype)


def causal_dwconv(x, w):
    K = w.shape[0]
    S = x.shape[1]
    xp = jnp.pad(x, ((0, 0), (K - 1, 0), (0, 0)))
    y = xp[:, 0:S] * w[0]
    for j in range(1, K):
        y = y + xp[:, j:j + S] * w[j]
    return y


def l2norm(x):
    return x * lax.rsqrt(jnp.sum(x * x, axis=-1, keepdims=True) + EPS)


def chunk_gated_delta_rule(q, k, v, g, beta):
    B, S, H, DK = q.shape
    DV = v.shape[-1]
    N = S // CHUNK

    def to_chunks(t):
        t = t.reshape((B, N, CHUNK, H) + t.shape[3:])
        return jnp.moveaxis(t, 3, 2)

    q = to_chunks(q * (DK ** -0.5))
    k = to_chunks(k)
    v = to_chunks(v)
    beta = to_chunks(beta)
    g = jnp.cumsum(to_chunks(g), axis=-1)
    idx = jnp.arange(CHUNK)
    causal = idx[:, None] >= idx[None, :]
    strict = idx[:, None] > idx[None, :]
    decay = jnp.exp(jnp.where(causal, g[..., :, None] - g[..., None, :], -jnp.inf))
    kb = k * beta[..., None]
    lower = jnp.where(strict, jnp.einsum('bnhid,bnhjd->bnhij', kb, k) * decay, 0.0)
    rhs = jnp.concatenate([v * beta[..., None], kb * jnp.exp(g)[..., None]], axis=-1)
    sol = lax.linalg.triangular_solve(lower, rhs, left_side=True, lower=True, unit_diagonal=True)
    u = sol[..., :DV]
    w = sol[..., DV:]
    a_intra = jnp.where(causal, jnp.einsum('bnhid,bnhjd->bnhij', q, k) * decay, 0.0)

    def step(state, xs):
        qc, kc, uc, wc, gc, ac = xs
        v_new = uc - jnp.einsum('bhck,bhkv->bhcv', wc, state)
        o = (jnp.einsum('bhck,bhkv->bhcv', qc * jnp.exp(gc)[..., None], state)
             + jnp.einsum('bhij,bhjv->bhiv', ac, v_new))
        g_last = gc[..., -1]
        k_dec = kc * jnp.exp(g_last[..., None] - gc)[..., None]
        state = state * jnp.exp(g_last)[..., None, None] + jnp.einsum('bhck,bhcv->bhkv', k_dec, v_new)
        return state, o

    xs = tuple(jnp.moveaxis(t, 1, 0) for t in (q, k, u, w, g, a_intra))
    state0 = jnp.zeros((B, H, DK, DV), jnp.float32)
    _, o = lax.scan(step, state0, xs)
    return jnp.transpose(o, (1, 0, 3, 2, 4)).reshape(B, S, H, DV)


def gdn_mixer(h, w_in, conv_w, a_log, dt_bias, norm_w, w_out):
    B, S, _ = h.shape
    proj = h @ w_in
    o_qkv = 2 * GDN_QK_DIM + GDN_V_DIM
    o_z = o_qkv + GDN_V_DIM
    qkv = jax.nn.silu(causal_dwconv(proj[..., :o_qkv], conv_w))
    z = proj[..., o_qkv:o_z]
    b = proj[..., o_z:o_z + GDN_HEADS]
    a = proj[..., o_z + GDN_HEADS:]
    q = l2norm(qkv[..., :GDN_QK_DIM].astype(jnp.float32).reshape(B, S, GDN_HEADS, GDN_DK))
    k = l2norm(qkv[..., GDN_QK_DIM:2 * GDN_QK_DIM].astype(jnp.float32).reshape(B, S, GDN_HEADS, GDN_DK))
    v = qkv[..., 2 * GDN_QK_DIM:].astype(jnp.float32).reshape(B, S, GDN_HEADS, GDN_DV)
    beta = jax.nn.sigmoid(b.astype(jnp.float32))
    g = -jnp.exp(a_log.astype(jnp.float32)) * jax.nn.softplus(a.astype(jnp.float32) + dt_bias.astype(jnp.float32))
    o = chunk_gated_delta_rule(q, k, v, g, beta)
    o = rmsnorm(o, norm_w) * jax.nn.silu(z.astype(jnp.float32).reshape(B, S, GDN_HEADS, GDN_DV))
    return o.reshape(B, S, GDN_V_DIM).astype(h.dtype) @ w_out


def diff_attention(h, w_q, k_sh, v_sh, lq1, lk1, lq2, lk2, subln_w, w_o, lambda_init):
    B, S, _ = h.shape
    q = (h @ w_q).reshape(B, S, DIFF_HEADS, 2, DIFF_DH)
    lam = (jnp.exp(jnp.sum(lq1.astype(jnp.float32) * lk1.astype(jnp.float32)))
           - jnp.exp(jnp.sum(lq2.astype(jnp.float32) * lk2.astype(jnp.float32))) + lambda_init)
    scale = DIFF_DH ** -0.5
    chunk_id = jnp.arange(S) // CHUNK
    outs = []
    for q0 in range(0, S, Q_BLOCK):
        k_end = q0 + Q_BLOCK
        s = jnp.einsum('bqhmd,bkhmd->bhmqk', q[:, q0:k_end], k_sh[:, :k_end]).astype(jnp.float32) * scale
        mask = chunk_id[None, :k_end] <= chunk_id[q0:k_end, None]
        p = jax.nn.softmax(jnp.where(mask, s, -jnp.inf), axis=-1)
        a = p[:, :, 0] - lam * p[:, :, 1]
        outs.append(jnp.einsum('bhqk,bkhe->bqhe', a.astype(v_sh.dtype), v_sh[:, :k_end]))
    o = jnp.concatenate(outs, axis=1)
    o = rmsnorm(o, subln_w) * (1.0 - lambda_init)
    return o.reshape(B, S, DIFF_V_DIM) @ w_o


def conv_ffn(h, w_up, conv_w, conv_b, w_down):
    gu = h @ w_up
    gate = causal_dwconv(gu[..., :D_FF], conv_w) + conv_b
    return (jax.nn.silu(gate) * gu[..., D_FF:]) @ w_down


def setup_inputs(seed: int = 0) -> dict:
    key = jax.random.key(seed)
    ks = jax.random.split(key, 24)
    f32 = jnp.float32

    def dense(k, shape, fan_in):
        return jax.random.normal(k, shape, f32) * (fan_in ** -0.5)

    def gain(k, shape):
        return 1.0 + 0.02 * jax.random.normal(k, shape, f32)

    dt = jnp.exp(jax.random.uniform(ks[9], (N_A, GDN_HEADS), f32, math.log(1e-3), math.log(1e-1)))
    return {
        'x': jax.random.normal(ks[0], (BATCH, SEQ, D_MODEL), f32),
        'norm_mix_pre': gain(ks[1], (DEPTH, D_MODEL)),
        'norm_mix_post': gain(ks[2], (DEPTH, D_MODEL)),
        'norm_ffn_pre': gain(ks[3], (DEPTH, D_MODEL)),
        'norm_ffn_post': gain(ks[4], (DEPTH, D_MODEL)),
        'gdn_w_in': dense(ks[5], (N_A, D_MODEL, GDN_IN_DIM), D_MODEL),
        'gdn_conv_w': dense(ks[6], (N_A, GDN_CONV, 2 * GDN_QK_DIM + GDN_V_DIM), GDN_CONV),
        'gdn_a_log': jnp.log(jax.random.uniform(ks[7], (N_A, GDN_HEADS), f32, 1.0, 16.0)),
        'gdn_dt_bias': dt + jnp.log(-jnp.expm1(-dt)),
        'gdn_norm_w': gain(ks[8], (N_A, GDN_DV)),
        'gdn_w_out': dense(ks[10], (N_A, GDN_V_DIM, D_MODEL), GDN_V_DIM),
        'kv_norm_w': gain(ks[11], (D_MODEL,)),
        'w_kv': dense(ks[12], (D_MODEL, DIFF_QK_DIM + DIFF_V_DIM), D_MODEL),
        'diff_w_q': dense(ks[13], (N_B, D_MODEL, DIFF_QK_DIM), D_MODEL),
        'diff_lq1': 0.1 * jax.random.normal(ks[14], (N_B, DIFF_DH), f32),
        'diff_lk1': 0.1 * jax.random.normal(ks[15], (N_B, DIFF_DH), f32),
        'diff_lq2': 0.1 * jax.random.normal(ks[16], (N_B, DIFF_DH), f32),
        'diff_lk2': 0.1 * jax.random.normal(ks[17], (N_B, DIFF_DH), f32),
        'diff_subln_w': gain(ks[18], (N_B, 2 * DIFF_DH)),
        'diff_w_o': dense(ks[19], (N_B, DIFF_V_DIM, D_MODEL), DIFF_V_DIM),
        'ffn_w_up': dense(ks[20], (DEPTH, D_MODEL, 2 * D_FF), D_MODEL),
        'ffn_conv_w': dense(ks[21], (DEPTH, FFN_CONV, D_FF), FFN_CONV),
        'ffn_conv_b': 0.01 * jax.random.normal(ks[22], (DEPTH, D_FF), f32),
        'ffn_w_down': dense(ks[23], (DEPTH, D_FF, D_MODEL), D_FF),
    }


def reference(x, norm_mix_pre, norm_mix_post, norm_ffn_pre, norm_ffn_post,
              gdn_w_in, gdn_conv_w, gdn_a_log, gdn_dt_bias, gdn_norm_w, gdn_w_out,
              kv_norm_w, w_kv, diff_w_q, diff_lq1, diff_lk1, diff_lq2, diff_lk2,
              diff_subln_w, diff_w_o, ffn_w_up, ffn_conv_w, ffn_conv_b, ffn_w_down):
    B, S, _ = x.shape
    h = x
    k_sh = None
    v_sh = None
    for layer in range(DEPTH):
        if layer == N_A:
            kv = rmsnorm(h, kv_norm_w) @ w_kv
            k_sh = kv[..., :DIFF_QK_DIM].reshape(B, S, DIFF_HEADS, 2, DIFF_DH)
            v_sh = kv[..., DIFF_QK_DIM:].reshape(B, S, DIFF_HEADS, 2 * DIFF_DH)
        hn = rmsnorm(h, norm_mix_pre[layer])
        if layer < N_A:
            m = gdn_mixer(hn, gdn_w_in[layer], gdn_conv_w[layer], gdn_a_log[layer],
                          gdn_dt_bias[layer], gdn_norm_w[layer], gdn_w_out[layer])
        else:
            j = layer - N_A
            lambda_init = 0.8 - 0.6 * math.exp(-0.3 * layer)
            m = diff_attention(hn, diff_w_q[j], k_sh, v_sh, diff_lq1[j], diff_lk1[j],
                               diff_lq2[j], diff_lk2[j], diff_subln_w[j], diff_w_o[j], lambda_init)
        h = h + rmsnorm(m, norm_mix_post[layer])
        f = conv_ffn(rmsnorm(h, norm_ffn_pre[layer]), ffn_w_up[layer], ffn_conv_w[layer],
                     ffn_conv_b[layer], ffn_w_down[layer])
        h = h + rmsnorm(f, norm_ffn_post[layer])
    return h
```

```python
import numpy as np
import ml_dtypes
from contextlib import ExitStack
import concourse.bass as bass
import concourse.mybir as mybir
from concourse.bass_utils import run_bass_kernel_spmd

F32 = mybir.dt.float32
BF16 = mybir.dt.bfloat16
AF = mybir.ActivationFunctionType
ALU = mybir.AluOpType
AX = mybir.AxisListType

D = 2048
KC = 16
TT = 512
H = 16
DFF = 5632
FC = 44
EPS = 1e-6
NEG = -30000.0
LAMBDA_INIT = 0.8 - 0.6 * float(np.exp(-0.3 * 1))


class Op:
    __slots__ = ("eng", "fn", "deps", "dma", "sig", "sigval", "semkey", "order")


class Sched:
    ENGS = ("pe", "act", "dve", "pool", "sp")

    def __init__(self):
        self.ops = []
        self.by_eng = {e: [] for e in self.ENGS}
        self.lastw = {}
        self.rd = {}
        self.bar = None
        self.absorbed = set()
        self.dma_since = []
        self.stream_ops = {}
        self.RING = 8

    def add(self, eng, fn, reads=(), writes=(), dma=None, nobar=False):
        op = Op()
        op.eng, op.fn, op.dma, op.sig, op.sigval, op.semkey = eng, fn, dma, False, 0, None
        op.order = len(self.ops)
        deps = set()
        for k in reads:
            w = self.lastw.get(k)
            if w is not None:
                deps.add(w)
        for k in writes:
            w = self.lastw.get(k)
            if w is not None:
                deps.add(w)
            for r in self.rd.get(k, {}).values():
                deps.add(r)
        if dma is not None:
            so = self.stream_ops.setdefault(dma, [])
            if len(so) >= self.RING:
                deps.add(so[-self.RING])
            so.append(op)
        if self.bar is not None and not nobar and eng not in self.absorbed:
            deps |= self.bar
            self.absorbed.add(eng)
        for k in writes:
            self.lastw[k] = op
            self.rd[k] = {}
        rk = eng if dma is None else ("dma", op.order)
        for k in reads:
            self.rd.setdefault(k, {})[rk] = op
        op.deps = deps
        self.ops.append(op)
        self.by_eng[eng].append(op)
        if dma is not None:
            self.dma_since.append(op)
        return op

    def barrier(self):
        last = set()
        for e in self.ENGS:
            for op in reversed(self.by_eng[e]):
                if op.dma is None:
                    last.add(op)
                    break
        last |= set(self.dma_since)
        self.dma_since = []
        self.bar = last
        self.absorbed = set()

    def emit(self, nc, es):
        for op in self.ops:
            for d in op.deps:
                if d.dma is not None or op.dma is not None or d.eng != op.eng or op.eng != "pe":
                    d.sig = True
        cnt = {}
        scount = {}
        for op in self.ops:
            if op.dma is not None:
                i = scount.get(op.dma, 0)
                scount[op.dma] = i + 1
                s = ("dma", op.dma, i % self.RING)
                cnt[s] = cnt.get(s, 0) + 16
                op.sigval, op.semkey = cnt[s], s
            elif op.sig:
                s = ("eng", op.eng)
                cnt[s] = cnt.get(s, 0) + 1
                op.sigval, op.semkey = cnt[s], s
        sems = {}
        for i, s in enumerate(sorted(cnt.keys(), key=str)):
            sems[s] = es.enter_context(nc.semaphore(f"sem{i}_" + "_".join(str(x) for x in s[1:])))
        self.maxcnt = dict(cnt)
        block = es.enter_context(nc.Block())

        def mk(engname):
            def body(e):
                waited = {}
                for op in self.by_eng[engname]:
                    need = {}
                    for d in op.deps:
                        if d.dma is None and op.dma is None and d.eng == op.eng and op.eng == "pe":
                            continue
                        if need.get(d.semkey, 0) < d.sigval:
                            need[d.semkey] = d.sigval
                    for s, v in need.items():
                        if waited.get(s, 0) < v:
                            e.wait_ge(sems[s], v)
                            waited[s] = v
                    ins = op.fn(e)
                    if op.dma is not None:
                        ins.then_inc(sems[op.semkey], 16)
                    elif op.sig:
                        ins.then_inc(sems[op.semkey], 1)
            return body

        block.tensor(mk("pe"))
        block.scalar(mk("act"))
        block.vector(mk("dve"))
        block.gpsimd(mk("pool"))
        block.sync(mk("sp"))


class Carver:
    def __init__(self, ap2d_bf16, nelem):
        self.t = ap2d_bf16
        self.n = nelem
        self.off = 0

    def reset(self, off=0):
        self.off = off

    def alloc(self, shape, dt):
        n = 1
        for s in shape:
            n *= s
        ne = n * 2 if dt == F32 else n
        ne = (ne + 1) // 2 * 2
        assert self.off + ne <= self.n, f"carver overflow {self.off}+{ne}>{self.n}"
        v = self.t[:, self.off:self.off + ne]
        self.off += ne
        if dt == F32:
            v = v.bitcast(F32)
        if len(shape) == 2:
            v = v.rearrange("p (a b) -> p a b", a=shape[0])
        elif len(shape) == 3:
            v = v.rearrange("p (a b c) -> p a b c", a=shape[0], b=shape[1])
        return v


def bc(ap, shape, axis):
    return ap.unsqueeze(axis).to_broadcast(list(shape))


class StopBuild(Exception):
    pass


def build_program(NT=8, OWN0=4, debug=False, do_l1=True, stop=99):
    nc = bass.Bass("TRN2", target_bir_lowering=False)
    SEXT = NT * TT
    NOWN = NT - OWN0
    SOWN = NOWN * TT
    NKB = SEXT // 128

    def din(name, shape, dt=F32):
        return nc.dram_tensor(name, list(shape), dt, kind="ExternalInput").ap()

    x = din("x", [SEXT, D])
    w_in = din("w_in", [D, 8224])
    w_out = din("w_out", [D, D])
    w_kv = din("w_kv", [D, 4096])
    w_q = din("w_q", [D, D])
    w_o = din("w_o", [D, D])
    w_up = [din("w_up0", [D, 2 * DFF]), din("w_up1", [D, 2 * DFF])]
    w_dn = [din("w_dn0", [DFF, D]), din("w_dn1", [DFF, D])]
    gp_d = din("gp", [128, 5 * 16])
    gpost_d = din("gpost", [4, D])
    convw_d = din("convw", [128, 48 * 4])
    fconvw_d = din("fconvw", [128, 2 * FC * 3])
    fconvb_d = din("fconvb", [128, 2 * FC])
    alog_d = din("alog", [16])
    dtb_d = din("dtb", [16])
    normw_d = din("normw", [128])
    subln_d = din("subln", [256])
    lq_d = din("lq", [4, 128])
    cmat_d = din("cmat", [6, 128, 128])
    maskb_d = din("maskb", [128, 32])
    out = nc.dram_tensor("out", [SOWN, D], F32, kind="ExternalOutput").ap()
    dbg = {}
    if debug:
        for nm, shp, dt in (("d_qT", [128, 8 * TT], BF16), ("d_kT", [128, 8 * TT], BF16), ("d_vtok", [128, 4 * 1024], BF16),
                            ("d_onT", [128, KC * TT], BF16), ("d_hmix", [128, 4 * D], F32), ("d_bg", [128, 4 * 16 * 3], F32),
                            ("d_oblk", [128, 1024], F32), ("d_TT", [128, 1024], BF16), ("d_AT", [128, 1024], BF16)):
            dbg[nm] = nc.dram_tensor(nm, shp, dt, kind="ExternalOutput").ap()
    HB = 1 if OWN0 > 0 else 0
    h1_d = nc.dram_tensor("h1_scr", [SOWN + HB * TT, D], F32).ap()
    KT_d = nc.dram_tensor("KT_scr", [16, 128, SEXT], BF16).ap()
    V_d = nc.dram_tensor("V_scr", [SEXT, D], BF16).ap()

    es = ExitStack()
    S = Sched()

    def sb(name, shape, dt):
        return es.enter_context(nc.sbuf_tensor("s_" + name, list(shape), dt))[:]

    cmat = sb("cmat", [128, 6, 128], F32)
    ident_f, NEGm, STRm, TRI_T, BLKm, ones_f = (cmat[:, i, :] for i in range(6))
    cmat_b = sb("cmat_b", [128, 2, 128], BF16)
    ident_b, ones_b = cmat_b[:, 0, :], cmat_b[:, 1, :]
    gp = sb("gp", [128, 5, 16], F32)
    convw = sb("convw", [128, 48, 4], F32)
    fconvw = sb("fconvw", [128, 2, FC, 3], F32)
    fconvb = sb("fconvb", [128, 2, FC], F32)
    alog_b = sb("alog_b", [128, 16], F32)
    dtb_b = sb("dtb_b", [128, 16], F32)
    normw_b = sb("normw_b", [128, 128], F32)
    subln_b = sb("subln_b", [128, 256], F32)
    lq_b = sb("lq_b", [128, 4, 128], F32)
    maskb = sb("maskb", [128, 32], F32)
    halo_qkv = sb("halo_qkv", [128, 48, 3], F32)
    halo_g = sb("halo_g", [128, FC, 2], F32)
    Sst = sb("Sst", [128, 16, 128], F32)
    Sbf = sb("Sbf", [128, 16, 128], BF16)
    wsl = [sb("wsl0", [128, 4096], BF16), sb("wsl1", [128, 4096], BF16)]
    wba = sb("wba", [128, 16, 32], BF16)
    hT = sb("hT", [128, KC, TT], BF16)
    onT = sb("onT", [128, KC, TT], BF16)
    h_tok = sb("h_tok", [128, 4, D], F32)
    junk_a = sb("junk_a", [128, D], BF16)
    smallf = sb("smallf", [128, 640], F32)
    ss = smallf[:, 0:4]
    rstd = smallf[:, 4:8]
    ssp = smallf[:, 8:24].rearrange("p (a b) -> p a b", a=4)
    beta = smallf[:, 32:96].rearrange("p (a b) -> p a b", a=4)
    gg = smallf[:, 96:160].rearrange("p (a b) -> p a b", a=4)
    gcs = smallf[:, 160:224].rearrange("p (a b) -> p a b", a=4)
    nbeta = smallf[:, 224:288].rearrange("p (a b) -> p a b", a=4)
    ekl = smallf[:, 288:352].rearrange("p (a b) -> p a b", a=4)
    negA = smallf[:, 352:368]
    ssh = smallf[:, 368:384]
    lam_t = smallf[:, 384:392]
    tmp16 = smallf[:, 400:464].rearrange("p (a b) -> p a b", a=4)
    RN = 49300
    Rt = sb("Rregion", [128, RN], BF16)
    R = Carver(Rt, RN)

    psum_t = es.enter_context(nc.psum_tensor("psum_all", [128, 8 * 512], F32))[:]

    def PS(b, nb=1):
        return psum_t[:, b * 512:(b + nb) * 512]

    def PSK(b, nb=1):
        return [("ps", i) for i in range(b, b + nb)]

    def dma(eng, out_ap, in_ap, reads, writes, stream, nobar=False):
        return S.add(eng, lambda e: e.dma_start(out=out_ap, in_=in_ap), reads=reads, writes=writes, dma=stream, nobar=nobar)

    dma("sp", cmat, cmat_d.rearrange("c p n -> p c n"), [], ["c_cmat"], "ld")
    dma("pool", cmat_b[:, 0, :], cmat_d[0], [], ["c_cmatb0"], "pl")
    dma("pool", cmat_b[:, 1, :], cmat_d[5], [], ["c_cmatb1"], "pl")
    dma("sp", gp, gp_d.rearrange("p (a b) -> p a b", a=5), [], ["c_gp"], "ld")
    dma("sp", convw, convw_d.rearrange("p (a b) -> p a b", a=48), [], ["c_convw"], "ld")
    dma("sp", fconvw, fconvw_d.rearrange("p (l a b) -> p l a b", l=2, a=FC), [], ["c_fconvw"], "ld")
    dma("sp", fconvb, fconvb_d.rearrange("p (l a) -> p l a", l=2), [], ["c_fconvb"], "ld")
    dma("sp", alog_b, alog_d.partition_broadcast(128), [], ["c_alog"], "ld")
    dma("sp", dtb_b, dtb_d.partition_broadcast(128), [], ["c_dtb"], "ld")
    dma("sp", normw_b, normw_d.partition_broadcast(128), [], ["c_normw"], "ld")
    dma("sp", subln_b, subln_d.partition_broadcast(128), [], ["c_subln"], "ld")
    for i in range(4):
        dma("sp", lq_b[:, i, :], lq_d[i].partition_broadcast(128), [], [("c_lq", i)], "ld")
    dma("sp", maskb, maskb_d, [], ["c_maskb"], "ld")
    dma("pool", wba, w_in[:, 8192:8224].rearrange("(k p) n -> p k n", p=128), [], ["c_wba"], "pl")
    CONST = ["c_cmat", "c_cmatb0", "c_cmatb1"]
    S.add("dve", lambda e: e.memset(halo_qkv, 0.0), writes=[("halo", c) for c in range(48)])
    S.add("dve", lambda e: e.memset(halo_g, 0.0), writes=[("halog", c) for c in range(FC)])
    S.add("dve", lambda e: e.memset(Sst, 0.0), writes=["Sst"])
    S.add("dve", lambda e: e.memset(Sbf, 0.0), writes=["Sbf"])
    S.add("act", lambda e: e.activation(negA, alog_b, AF.Exp), reads=["c_alog"], writes=["negA"])
    S.add("dve", lambda e: e.tensor_scalar(negA, negA, -1.0, None, op0=ALU.mult), reads=["negA"], writes=["negA"])

    wctr = [0]

    def load_w(wsrc, k0, nk, c0, ncols):
        s = wctr[0] % 2
        wctr[0] += 1
        view = wsl[s][:, 0:nk * ncols].rearrange("p (k n) -> p k n", k=nk)
        src = wsrc[k0 * 128:(k0 + nk) * 128, c0:c0 + ncols].rearrange("(k p) n -> p k n", p=128)
        dma("pool", view, src, [], [("wsl", s)], f"w{s}", nobar=True)
        return view, ("wsl", s)

    def rmsnorm_T(hkeys, gidx, dstT, dkey, hnb, ps_banks=(0, 1)):
        for blk in range(4):
            S.add("act", lambda e, blk=blk: e.activation(junk_a, h_tok[:, blk, :], AF.Square, accum_out=ss[:, blk:blk + 1]),
                  reads=[hkeys[blk]], writes=[("ss", blk)])
            S.add("act", lambda e, blk=blk: e.activation(rstd[:, blk:blk + 1], ss[:, blk:blk + 1], AF.Sqrt, scale=1.0 / D, bias=EPS),
                  reads=[("ss", blk)], writes=[("rstd", blk)])
            S.add("dve", lambda e, blk=blk: e.reciprocal(rstd[:, blk:blk + 1], rstd[:, blk:blk + 1]),
                  reads=[("rstd", blk)], writes=[("rstd", blk)])
            sl = blk % 2
            S.add("dve", lambda e, blk=blk, sl=sl: e.tensor_scalar(hnb[:, sl, :], h_tok[:, blk, :], rstd[:, blk:blk + 1], None, op0=ALU.mult),
                  reads=[hkeys[blk], ("rstd", blk)], writes=[("hnb", sl)])
            for half in range(2):
                b = ps_banks[half]
                pv = PS(b).bitcast(BF16).rearrange("p (a b) -> p a b", a=8)

                def tr(e, half=half, sl=sl, pv=pv):
                    ins = None
                    for j in range(8):
                        kc = half * 8 + j
                        ins = e.transpose(pv[:, j, :], hnb[:, sl, kc * 128:(kc + 1) * 128], ident_b)
                    return ins
                S.add("pe", tr, reads=[("hnb", sl), "c_cmatb0"], writes=PSK(b))
                for j in range(8):
                    kc = half * 8 + j
                    dst = dstT[:, kc, blk * 128:(blk + 1) * 128]
                    if half == 0:
                        S.add("act", lambda e, dst=dst, pv=pv, j=j, kc=kc: e.activation(dst, pv[:, j, :], AF.Copy, scale=gp[:, gidx, kc:kc + 1]),
                              reads=PSK(b) + ["c_gp"], writes=[(dkey, blk, half)])
                    else:
                        S.add("dve", lambda e, dst=dst, pv=pv, j=j, kc=kc: e.tensor_scalar(dst, pv[:, j, :], gp[:, gidx, kc:kc + 1], None, op0=ALU.mult),
                              reads=PSK(b) + ["c_gp"], writes=[(dkey, blk, half)])

    pctr = [0]

    def proj_chunks(wsrc, nk, actT, akeys, chunk_cols, consume, gcols=256, banks=(2, 3, 4, 5), lag=2):
        parts = [(k0, min(16, nk - k0)) for k0 in range(0, nk, 16)]
        pending = []

        def tick():
            for it in pending:
                it[0] -= 1
            ready = [it for it in pending if it[0] <= 0]
            pending[:] = [it for it in pending if it[0] > 0]
            for _, fn in ready:
                nxt = fn()
                if nxt is not None:
                    pending.append([lag, nxt])

        i = 0
        while i < len(chunk_cols):
            c0 = chunk_cols[i][0]
            run, tot = [], 0
            for cc in chunk_cols[i:]:
                if cc[0] == c0 + tot and tot + cc[1] <= gcols:
                    run.append(cc)
                    tot += cc[1]
                else:
                    break
            bks = []
            for _ in run:
                bks.append(banks[pctr[0] % len(banks)])
                pctr[0] += 1
            for pi, (k0, nkp) in enumerate(parts):
                wv, wkey = load_w(wsrc, k0, nkp, c0, tot)
                off = 0
                for ri, (cc0, cn, tag) in enumerate(run):
                    b = bks[ri]

                    def mm(e, wv=wv, off=off, cn=cn, b=b, k0=k0, nkp=nkp, pi=pi):
                        ins = None
                        for kc in range(nkp):
                            ins = e.matmul(PS(b)[0:cn, :], lhsT=wv[:, kc, off:off + cn], rhs=actT[:, k0 + kc, :],
                                           start=(pi == 0 and kc == 0), stop=(pi == len(parts) - 1 and kc == nkp - 1))
                        return ins
                    S.add("pe", mm, reads=[wkey] + list(akeys), writes=PSK(b))
                    off += cn
                    if pi == len(parts) - 1:
                        nxt = consume(tag, b, cn)
                        if nxt is not None:
                            pending.append([lag + 1, nxt])
                    tick()
            i += len(run)
        while pending:
            _, fn = pending.pop(0)
            nxt = fn()
            if nxt is not None:
                pending.append([0, nxt])

    def transposes_pe(srcT_ap, skey, dt, bank):
        if dt == BF16:
            pv = PS(bank).bitcast(BF16)[:, 0:512].rearrange("p (a b) -> p a b", a=4)
            idn, ik = ident_b, "c_cmatb0"
        else:
            pv = PS(bank).rearrange("p (a b) -> p a b", a=4)
            idn, ik = ident_f, "c_cmat"

        def tr(e):
            ins = None
            for blk in range(4):
                ins = e.transpose(pv[:, blk, :], srcT_ap[:, blk * 128:(blk + 1) * 128], idn)
            return ins
        S.add("pe", tr, reads=[skey, ik], writes=PSK(bank))
        return pv, PSK(bank)

    def transposes_to_tok(srcT_ap, skey, dst_fn, dkeys, dt, bank, eng_evac="act"):
        if dt == BF16:
            pv = PS(bank).bitcast(BF16)[:, 0:512].rearrange("p (a b) -> p a b", a=4)
            idn, ik = ident_b, "c_cmatb0"
        else:
            pv = PS(bank).rearrange("p (a b) -> p a b", a=4)
            idn, ik = ident_f, "c_cmat"

        def tr(e):
            ins = None
            for blk in range(4):
                ins = e.transpose(pv[:, blk, :], srcT_ap[:, blk * 128:(blk + 1) * 128], idn)
            return ins
        S.add("pe", tr, reads=[skey, ik], writes=PSK(bank))
        dst_fn(pv, PSK(bank))

    R.reset()
    qT = R.alloc([8, TT], BF16)
    kT = R.alloc([8, TT], BF16)
    vtok = R.alloc([4, 1024], BF16)
    ktok = R.alloc([4, 1024], BF16)
    zstok = R.alloc([4, 1024], BF16)
    base_AB = R.off
    R_hnb = R.alloc([2, D], BF16)
    pc = R.alloc([3, 516], F32)
    acc = R.alloc([6, TT], F32)
    sq = R.alloc([3, TT], BF16)
    rt = R.alloc([3, TT], F32)
    vzT = R.alloc([3, TT], BF16)
    zT = R.alloc([3, TT], BF16)
    R.reset(base_AB)
    rhs_d = R.alloc([8, 128], F32)
    DT = R.alloc([8, 128], F32)
    EG = R.alloc([8, 128], F32)
    AT = R.alloc([8, 128], BF16)
    Zf = R.alloc([8, 128], F32)
    Qf = R.alloc([8, 128], F32)
    Qb = R.alloc([8, 128], BF16)
    Xb = [R.alloc([8, 128], BF16), R.alloc([8, 128], BF16)]
    Wb = [R.alloc([8, 128], BF16), R.alloc([8, 128], BF16)]
    kdT = R.alloc([8, 128], BF16)
    qdT = R.alloc([8, 128], BF16)
    kdec = R.alloc([8, 128], BF16)
    rr = R.alloc([8, 128], BF16)
    vnew = R.alloc([8, 128], BF16)
    o_blk = R.alloc([8, 128], F32)
    tmpf = R.alloc([8, 128], F32)
    on_tok = R.alloc([8, 128], BF16)
    R.reset()
    mtok = R.alloc([4, D], F32)
    gain_bc = R.alloc([D], F32)
    base_D = R.off
    R_hnb2 = R.alloc([2, D], BF16)
    vtok2 = R.alloc([4, D], BF16)
    R.reset(base_D)
    actT = R.alloc([FC, TT], BF16)
    base_T = R.off
    gpre = R.alloc([3, 516], F32)
    gacc = R.alloc([3, TT], F32)
    R.reset(base_T)
    mT = R.alloc([3, TT], F32)
    kvT = R.alloc([3, TT], BF16)

    ALLH = [("h", b) for b in range(4)]

    def layer0_tile(t):
        for blk in range(4):
            dma("sp", h_tok[:, blk, :], x[t * TT + blk * 128:t * TT + (blk + 1) * 128, :], [], [("h", blk)], "ld")
        if stop <= 0:
            raise StopBuild()
        rmsnorm_T(ALLH, 0, hT, "hT", R_hnb)
        if stop <= 1:
            raise StopBuild()
        for blk in range(4):
            def mmba(e, blk=blk):
                ins = None
                for kc in range(KC):
                    ins = e.matmul(PS(6)[:, blk * 32:(blk + 1) * 32], lhsT=hT[:, kc, blk * 128:(blk + 1) * 128], rhs=wba[:, kc, :],
                                   start=(kc == 0), stop=(kc == KC - 1))
                return ins
            S.add("pe", mmba, reads=HTK + ["c_wba"], writes=PSK(6))
        pba = PS(6)[:, 0:128].rearrange("p (a b) -> p a b", a=4)
        S.add("act", lambda e: e.activation(beta, pba[:, :, 0:16], AF.Sigmoid), reads=PSK(6), writes=["beta"])
        S.add("dve", lambda e: e.tensor_tensor(tmp16, pba[:, :, 16:32], bc(dtb_b, [128, 4, 16], 1), op=ALU.add),
              reads=PSK(6) + ["c_dtb"], writes=["tmp16"])
        S.add("act", lambda e: e.activation(tmp16, tmp16, AF.Exp), reads=["tmp16"], writes=["tmp16"])
        S.add("act", lambda e: e.activation(tmp16, tmp16, AF.Ln, bias=1.0), reads=["tmp16"], writes=["tmp16"])
        S.add("dve", lambda e: e.tensor_tensor(gg, tmp16, bc(negA, [128, 4, 16], 1), op=ALU.mult),
              reads=["tmp16", "negA"], writes=["gg"])
        S.add("dve", lambda e: e.tensor_scalar(nbeta, beta, -1.0, None, op0=ALU.mult), reads=["beta"], writes=["nbeta"])
        for blk in range(4):
            S.add("pe", lambda e, blk=blk: e.matmul(PS(7)[:, blk * 32:blk * 32 + 16], lhsT=TRI_T, rhs=gg[:, blk, :], start=True, stop=True),
                  reads=["gg", "c_cmat"], writes=PSK(7))
            S.add("pe", lambda e, blk=blk: e.matmul(PS(7)[:, blk * 32 + 16:blk * 32 + 32], lhsT=BLKm, rhs=gg[:, blk, :], start=True, stop=True),
                  reads=["gg", "c_cmat"], writes=PSK(7))
        pg = PS(7)[:, 0:128].rearrange("p (a b) -> p a b", a=4)
        S.add("dve", lambda e: e.tensor_copy(gcs, pg[:, :, 0:16]), reads=PSK(7), writes=["gcs"])
        S.add("dve", lambda e: e.tensor_tensor(ekl, pg[:, :, 16:32], gcs, op=ALU.subtract), reads=PSK(7) + ["gcs"], writes=["ekl"])
        S.add("act", lambda e: e.activation(ekl, ekl, AF.Exp), reads=["ekl"], writes=["ekl"])
        if debug and t == 0:
            pass

        if stop <= 2:
            raise StopBuild()
        for hb in range(2):
            S.barrier()
            phaseA(t, hb)
            if stop <= 3:
                raise StopBuild()
            S.barrier()
            phaseB(t, hb)
            if stop <= 4:
                raise StopBuild()
        if stop <= 4.5:
            raise StopBuild()
        S.barrier()
        if debug and t == 0:
            dma("sp", dbg["d_onT"], onT.rearrange("p a b -> p (a b)"), ["onT"], ["d_onT"], "st")
        if stop <= 4.7:
            raise StopBuild()
        phaseC(w_out, KC, onT, ["onT"], 0)
        if stop <= 5:
            raise StopBuild()
        if debug and t == 0:
            dma("sp", dbg["d_hmix"], h_tok.rearrange("p a b -> p (a b)"), ALLH, ["d_hmix"], "st")
        ffn(0)

    HTK = [("hT", b_, h_) for b_ in range(4) for h_ in range(2)]

    def phaseA(t, hb):
        cols = []
        for typ, base in (("q", 0), ("k", 2048), ("v", 4096), ("z", 6144)):
            for h8 in range(8):
                cols.append((base + (8 * hb + h8) * 128, 128, (typ, h8)))
        slot = [0]

        tbank = [0]

        def next_tbank():
            tbank[0] += 1
            return tbank[0] % 2

        def consume(tag, b, cn):
            typ, h8 = tag
            hd = 8 * hb + h8
            n = slot[0]
            slot[0] += 1
            s = n % 3
            sa = n % 6
            if typ == "z":
                S.add("act", lambda e, s=s, b=b: e.activation(zT[:, s, :], PS(b), AF.Silu), reads=PSK(b), writes=[("zT", s)])

                def st_tr():
                    pv, pk = transposes_pe(zT[:, s, :], ("zT", s), BF16, next_tbank())
                    S.add("dve", lambda e: e.tensor_copy(zstok[:, :, h8 * 128:(h8 + 1) * 128], pv), reads=pk, writes=["zstok"])
                    return None
                return st_tr
            c = {"q": 0, "k": 16, "v": 32}[typ] + hd
            S.add("dve", lambda e, s=s, c=c: e.tensor_copy(pc[:, s, 0:3], halo_qkv[:, c, :]), reads=[("halo", c)], writes=[("pc", s)])
            S.add("act", lambda e, s=s, b=b: e.activation(pc[:, s, 3:515], PS(b), AF.Copy), reads=PSK(b), writes=[("pc", s)])

            def st_conv():
                S.add("dve", lambda e, s=s, c=c: e.tensor_copy(halo_qkv[:, c, :], pc[:, s, 512:515]), reads=[("pc", s)], writes=[("halo", c)])
                S.add("dve", lambda e: e.tensor_scalar(acc[:, sa, :], pc[:, s, 0:512], convw[:, c, 0:1], None, op0=ALU.mult),
                      reads=[("pc", s), "c_convw"], writes=[("acc", sa)])
                for j in range(1, 4):
                    S.add("dve", lambda e, j=j: e.scalar_tensor_tensor(acc[:, sa, :], pc[:, s, j:j + 512], convw[:, c, j:j + 1], acc[:, sa, :], op0=ALU.mult, op1=ALU.add),
                          reads=[("pc", s), "c_convw", ("acc", sa)], writes=[("acc", sa)])
                return st_silu

            def st_silu():
                if typ == "v":
                    S.add("act", lambda e: e.activation(vzT[:, s, :], acc[:, sa, :], AF.Silu), reads=[("acc", sa)], writes=[("vzT", s)])

                    def st_trv():
                        pv, pk = transposes_pe(vzT[:, s, :], ("vzT", s), BF16, next_tbank())
                        S.add("dve", lambda e: e.tensor_copy(vtok[:, :, h8 * 128:(h8 + 1) * 128], pv), reads=pk, writes=["vtok"])
                        return None
                    return st_trv
                S.add("act", lambda e: e.activation(acc[:, sa, :], acc[:, sa, :], AF.Silu), reads=[("acc", sa)], writes=[("acc", sa)])
                S.add("act", lambda e: e.activation(sq[:, s, :], acc[:, sa, :], AF.Square), reads=[("acc", sa)], writes=[("sq", s)])
                return st_ones

            dst = (qT if typ == "q" else kT)[:, h8, :]
            dk = ("qT" if typ == "q" else "kT", h8)
            b2box = [None]

            def st_ones():
                b2 = 6 + next_tbank()
                b2box[0] = b2
                S.add("pe", lambda e: e.matmul(PS(b2), lhsT=ones_b, rhs=sq[:, s, :], start=True, stop=True),
                      reads=[("sq", s), "c_cmatb1"], writes=PSK(b2))
                return st_sqrt()

            def st_sqrt():
                b2 = b2box[0]
                if typ == "q":
                    S.add("act", lambda e: e.activation(rt[:, s, :], PS(b2), AF.Sqrt, scale=128.0, bias=128.0 * EPS), reads=PSK(b2), writes=[("rt", s)])
                else:
                    S.add("act", lambda e: e.activation(rt[:, s, :], PS(b2), AF.Sqrt, bias=EPS), reads=PSK(b2), writes=[("rt", s)])
                return st_norm

            def st_norm():
                S.add("dve", lambda e: e.reciprocal(rt[:, s, :], rt[:, s, :]), reads=[("rt", s)], writes=[("rt", s)])
                S.add("dve", lambda e: e.tensor_tensor(dst, acc[:, sa, :], rt[:, s, :], op=ALU.mult), reads=[("acc", sa), ("rt", s)], writes=[dk])
                if typ != "k":
                    return None

                def st_trk():
                    pv, pk = transposes_pe(kT[:, h8, :], dk, BF16, next_tbank())
                    S.add("act", lambda e: e.activation(ktok[:, :, h8 * 128:(h8 + 1) * 128], pv, AF.Copy), reads=pk, writes=["ktok"])
                    return None
                return st_trk
            return st_conv
        proj_chunks(w_in, KC, hT, HTK, cols, consume, lag=1)
        if debug and t == 0 and hb == 0:
            dma("sp", dbg["d_qT"], qT.rearrange("p a b -> p (a b)"), [("qT", i) for i in range(8)], ["d_qT"], "st")
            dma("sp", dbg["d_kT"], kT.rearrange("p a b -> p (a b)"), [("kT", i) for i in range(8)], ["d_kT"], "st")
            dma("sp", dbg["d_vtok"], vtok.rearrange("p a b -> p (a b)"), ["vtok"], ["d_vtok"], "st")
            dma("sp", dbg["d_bg"][:, 0:64], beta.rearrange("p a b -> p (a b)"), ["beta"], ["d_bg0"], "st")
            dma("sp", dbg["d_bg"][:, 64:128], gg.rearrange("p a b -> p (a b)"), ["gg"], ["d_bg1"], "st")
            dma("sp", dbg["d_bg"][:, 128:192], gcs.rearrange("p a b -> p (a b)"), ["gcs"], ["d_bg2"], "st")

    def phaseB(t, hb):
        QK = [("qT", i) for i in range(8)] + [("kT", i) for i in range(8)]
        h0 = 8 * hb
        r2 = lambda ap: ap.rearrange("p a b -> p (a b)")
        if t == 0 and hb == 0:
            pass
        S.add("dve", lambda e: e.memset(r2(rr), 0.0), writes=["rr"])
        S.add("dve", lambda e: e.memset(r2(vnew), 0.0), writes=["vnew"])
        for blk in range(4):
            cs = slice(blk * 128, (blk + 1) * 128)
            S.add("dve", lambda e, blk=blk: e.tensor_tensor(rhs_d, bc(ident_f, [128, 8, 128], 1), bc(gcs[:, blk, h0:h0 + 8], [128, 8, 128], 2), op=ALU.mult),
                  reads=["gcs", "c_cmat"], writes=["rhs_d"])

            def mmrow(e):
                e.matmul(PS(4), lhsT=ones_f, rhs=r2(rhs_d)[:, 0:512], start=True, stop=True)
                return e.matmul(PS(5), lhsT=ones_f, rhs=r2(rhs_d)[:, 512:1024], start=True, stop=True)
            S.add("pe", mmrow, reads=["rhs_d", "c_cmat"], writes=PSK(4, 2))
            gcrow = PS(4, 2).rearrange("p (a b) -> p a b", a=8)

            def gram(e, cs=cs):
                ins = None
                for h8 in range(8):
                    o0 = h8 * 256
                    e.matmul(PS(0, 4)[:, o0:o0 + 128], lhsT=kT[:, h8, cs], rhs=kT[:, h8, cs], start=True, stop=True)
                    ins = e.matmul(PS(0, 4)[:, o0 + 128:o0 + 256], lhsT=kT[:, h8, cs], rhs=qT[:, h8, cs], start=True, stop=True)
                return ins
            S.add("pe", gram, reads=QK, writes=PSK(0, 4))
            G = PS(0, 4).rearrange("p (a b) -> p a b", a=8)
            S.add("act", lambda e: e.activation(EG, gcrow, AF.Exp), reads=PSK(4, 2), writes=["EG"])
            S.add("dve", lambda e, blk=blk: e.tensor_tensor(DT, gcrow, bc(gcs[:, blk, h0:h0 + 8], [128, 8, 128], 2), op=ALU.subtract),
                  reads=PSK(4, 2) + ["gcs"], writes=["DT"])
            S.add("dve", lambda e: e.tensor_tensor(DT, DT, bc(NEGm, [128, 8, 128], 1), op=ALU.add), reads=["DT", "c_cmat"], writes=["DT"])
            S.add("act", lambda e: e.activation(DT, DT, AF.Exp), reads=["DT"], writes=["DT"])
            S.add("dve", lambda e: e.tensor_tensor(AT, G[:, :, 128:256], DT, op=ALU.mult), reads=PSK(0, 4) + ["DT"], writes=["AT"])
            S.add("dve", lambda e: e.tensor_tensor(DT, DT, bc(STRm, [128, 8, 128], 1), op=ALU.mult), reads=["DT", "c_cmat"], writes=["DT"])
            S.add("dve", lambda e: e.tensor_tensor(Zf, G[:, :, 0:128], DT, op=ALU.mult), reads=PSK(0, 4) + ["DT"], writes=["Zf"])
            S.add("dve", lambda e, blk=blk: e.tensor_tensor(Zf, Zf, bc(nbeta[:, blk, h0:h0 + 8], [128, 8, 128], 2), op=ALU.mult),
                  reads=["Zf", "nbeta"], writes=["Zf"])
            S.add("dve", lambda e: e.tensor_tensor(Qf, Zf, bc(ident_f, [128, 8, 128], 1), op=ALU.add), reads=["Zf", "c_cmat"], writes=["Qf"])
            S.add("act", lambda e: e.activation(Xb[0], Zf, AF.Copy), reads=["Zf"], writes=[("Xb", 0)])
            S.add("act", lambda e: e.activation(Qb, Qf, AF.Copy), reads=["Qf"], writes=["Qb"])
            S.add("dve", lambda e, cs=cs: e.tensor_tensor(kdT, kT[:, :, cs], EG, op=ALU.mult), reads=QK + ["EG"], writes=["kdT"])
            S.add("dve", lambda e, cs=cs: e.tensor_tensor(qdT, qT[:, :, cs], EG, op=ALU.mult), reads=QK + ["EG"], writes=["qdT"])
            S.add("dve", lambda e, blk=blk: e.tensor_tensor(kdec, ktok[:, blk, :].rearrange("p (a b) -> p a b", a=8),
                                                            bc(ekl[:, blk, h0:h0 + 8], [128, 8, 128], 2), op=ALU.mult),
                  reads=["ktok", "ekl"], writes=["kdec"])
            pvb = PS(6).bitcast(BF16).rearrange("p (a b) -> p a b", a=8)

            def trz(e):
                ins = None
                for h8 in range(8):
                    ins = e.transpose(pvb[:, h8, :], Xb[0][:, h8, :], ident_b)
                return ins
            S.add("pe", trz, reads=[("Xb", 0), "c_cmatb0"], writes=PSK(6))
            S.add("dve", lambda e: e.tensor_copy(Wb[0], pvb), reads=PSK(6), writes=[("Wb", 0)])
            for n in range(5):
                xi, xo = n % 2, (n + 1) % 2
                if n < 4:
                    def mmx(e, xi=xi):
                        ins = None
                        for h8 in range(8):
                            ins = e.matmul(PS(0, 2)[:, h8 * 128:(h8 + 1) * 128], lhsT=Wb[xi][:, h8, :], rhs=Xb[xi][:, h8, :], start=True, stop=True)
                        return ins
                    S.add("pe", mmx, reads=[("Xb", xi), ("Wb", xi)], writes=PSK(0, 2))

                def mmw(e, xi=xi):
                    ins = None
                    for h8 in range(8):
                        ins = e.matmul(PS(2, 2)[:, h8 * 128:(h8 + 1) * 128], lhsT=Xb[xi][:, h8, :], rhs=Wb[xi][:, h8, :], start=True, stop=True)
                    return ins
                S.add("pe", mmw, reads=[("Xb", xi), ("Wb", xi)], writes=PSK(2, 2))
                if n < 4:
                    S.add("act", lambda e, xo=xo: e.activation(r2(Xb[xo]), PS(0, 2), AF.Copy), reads=PSK(0, 2), writes=[("Xb", xo)])
                S.add("dve", lambda e, xo=xo: e.tensor_copy(r2(Wb[xo]), PS(2, 2)), reads=PSK(2, 2), writes=[("Wb", xo)])

                def mmq(e, xo=xo):
                    ins = None
                    for h8 in range(8):
                        ins = e.matmul(PS(6, 2)[:, h8 * 128:(h8 + 1) * 128], lhsT=Wb[xo][:, h8, :], rhs=Qb[:, h8, :], start=True, stop=True)
                    return ins
                S.add("pe", mmq, reads=[("Wb", xo), "Qb"], writes=PSK(6, 2))
                S.add("dve", lambda e: e.tensor_tensor(r2(Qf), r2(Qf), PS(6, 2), op=ALU.add), reads=PSK(6, 2) + ["Qf"], writes=["Qf"])
                S.add("act", lambda e: e.activation(Qb, Qf, AF.Copy), reads=["Qf"], writes=["Qb"])
            if debug and t == 0 and hb == 0 and blk == 0:
                dma("sp", dbg["d_TT"], r2(Qb), ["Qb"], ["d_TT"], "st")
                dma("sp", dbg["d_AT"], r2(AT), ["AT"], ["d_AT"], "st")
            for c in range(2):
                ps_ = slice(64 * c, 64 * c + 64)

                def mmP(e):
                    ins = None
                    for h8 in range(8):
                        ins = e.matmul(PS(0, 2)[:, h8 * 128:(h8 + 1) * 128], lhsT=kdT[:, h8, :], rhs=Sbf[:, h0 + h8, :], start=True, stop=True)
                    return ins
                S.add("pe", mmP, reads=["kdT", "Sbf"], writes=PSK(0, 2))
                S.add("dve", lambda e, ps_=ps_, blk=blk: e.tensor_tensor(r2(rr)[ps_, :], vtok[ps_, blk, :], PS(0, 2)[ps_, :], op=ALU.subtract),
                      reads=PSK(0, 2) + ["vtok"], writes=["rr"])

                def mmV(e):
                    ins = None
                    for h8 in range(8):
                        ins = e.matmul(PS(2, 2)[:, h8 * 128:(h8 + 1) * 128], lhsT=Qb[:, h8, :], rhs=rr[:, h8, :], start=True, stop=True)
                    return ins
                S.add("pe", mmV, reads=["Qb", "rr"], writes=PSK(2, 2))
                S.add("dve", lambda e, ps_=ps_, blk=blk: e.tensor_tensor(vnew[ps_], PS(2, 2)[ps_, :].rearrange("p (a b) -> p a b", a=8),
                                                                         bc(beta[ps_, blk, h0:h0 + 8], [64, 8, 128], 2), op=ALU.mult),
                      reads=PSK(2, 2) + ["beta"], writes=["vnew"])

                def mmO(e):
                    ins = None
                    for h8 in range(8):
                        e.matmul(PS(4, 2)[:, h8 * 128:(h8 + 1) * 128], lhsT=qdT[:, h8, :], rhs=Sbf[:, h0 + h8, :], start=True, stop=False)
                        ins = e.matmul(PS(4, 2)[:, h8 * 128:(h8 + 1) * 128], lhsT=AT[:, h8, :], rhs=vnew[:, h8, :], start=False, stop=True)
                    return ins
                S.add("pe", mmO, reads=["qdT", "Sbf", "AT", "vnew"], writes=PSK(4, 2))
                S.add("act", lambda e, ps_=ps_: e.activation(r2(o_blk)[ps_, :], PS(4, 2)[ps_, :], AF.Copy), reads=PSK(4, 2), writes=["o_blk"])

                def mmS(e, ps_=ps_):
                    ins = None
                    for h8 in range(8):
                        ins = e.matmul(PS(6, 2)[:, h8 * 128:(h8 + 1) * 128], lhsT=kdec[ps_, h8, :], rhs=vnew[ps_, h8, :], start=True, stop=True)
                    return ins
                S.add("pe", mmS, reads=["kdec", "vnew"], writes=PSK(6, 2))
                lastc = 64 * c + 63
                S.add("dve", lambda e, lastc=lastc: e.tensor_tensor(Sst[:, h0:h0 + 8, :], Sst[:, h0:h0 + 8, :], bc(EG[:, :, lastc], [128, 8, 128], 2), op=ALU.mult),
                      reads=["Sst", "EG"], writes=["Sst"])
                S.add("dve", lambda e: e.tensor_tensor(Sst[:, h0:h0 + 8, :], Sst[:, h0:h0 + 8, :], PS(6, 2).rearrange("p (a b) -> p a b", a=8), op=ALU.add),
                      reads=["Sst"] + PSK(6, 2), writes=["Sst"])
                S.add("act", lambda e: e.activation(Sbf[:, h0:h0 + 8, :], Sst[:, h0:h0 + 8, :], AF.Copy), reads=["Sst"], writes=["Sbf"])
            if debug and t == 0 and hb == 0 and blk == 0:
                dma("sp", dbg["d_oblk"], r2(o_blk), ["o_blk"], ["d_oblk"], "st")
            S.add("dve", lambda e: e.tensor_tensor(tmpf, o_blk, o_blk, op=ALU.mult), reads=["o_blk"], writes=["tmpf"])
            S.add("dve", lambda e: e.reduce_sum(ssh[:, 0:8], tmpf, axis=AX.X), reads=["tmpf"], writes=["ssh"])
            S.add("act", lambda e: e.activation(ssh[:, 0:8], ssh[:, 0:8], AF.Sqrt, scale=1.0 / 128, bias=EPS), reads=["ssh"], writes=["ssh"])
            S.add("dve", lambda e: e.reciprocal(ssh[:, 0:8], ssh[:, 0:8]), reads=["ssh"], writes=["ssh"])
            S.add("dve", lambda e: e.tensor_tensor(tmpf, o_blk, bc(ssh[:, 0:8], [128, 8, 128], 2), op=ALU.mult), reads=["o_blk", "ssh"], writes=["tmpf"])
            S.add("dve", lambda e: e.tensor_tensor(tmpf, tmpf, bc(normw_b, [128, 8, 128], 1), op=ALU.mult), reads=["tmpf", "c_normw"], writes=["tmpf"])
            S.add("dve", lambda e, blk=blk: e.tensor_tensor(on_tok, tmpf, zstok[:, blk, :].rearrange("p (a b) -> p a b", a=8), op=ALU.mult),
                  reads=["tmpf", "zstok"], writes=["on_tok"])
            pvo = PS(0).bitcast(BF16).rearrange("p (a b) -> p a b", a=8)

            def tro(e):
                ins = None
                for h8 in range(8):
                    ins = e.transpose(pvo[:, h8, :], on_tok[:, h8, :], ident_b)
                return ins
            S.add("pe", tro, reads=["on_tok", "c_cmatb0"], writes=PSK(0))
            S.add("act", lambda e, cs=cs: e.activation(onT[:, h0:h0 + 8, cs], pvo, AF.Copy), reads=PSK(0), writes=["onT"])

    def phaseC(wsrc, nk, actT_, akey, gpost_idx):
        dma("sp", gain_bc, gpost_d[gpost_idx].partition_broadcast(128), [], ["gain_bc"], "ld")
        cols = [(c * 128, 128, c) for c in range(KC)]
        slot = [0]

        def consume(tag, b, cn):
            c = tag
            s = slot[0] % 3
            bank = slot[0] % 2
            slot[0] += 1
            S.add("act", lambda e, s=s, b=b: e.activation(mT[:, s, :], PS(b), AF.Copy), reads=PSK(b), writes=[("mT", s)])

            def evm(pv, pk, c=c):
                S.add("dve", lambda e: e.tensor_copy(mtok[:, :, c * 128:(c + 1) * 128], pv), reads=pk, writes=[("mtok", b_) for b_ in range(4)])
            def st2():
                pv, pk = transposes_pe(mT[:, s, :], ("mT", s), F32, bank)
                evm(pv, pk)
                return None
            return st2
        proj_chunks(wsrc, nk, actT_, akey, cols, consume, lag=1)
        if stop <= 4.95:
            raise StopBuild()
        for blk in range(4):
            S.add("act", lambda e, blk=blk: e.activation(junk_a, mtok[:, blk, :], AF.Square, accum_out=ss[:, blk:blk + 1]),
                  reads=[("mtok", blk)], writes=[("ss", blk)])
            S.add("act", lambda e, blk=blk: e.activation(rstd[:, blk:blk + 1], ss[:, blk:blk + 1], AF.Sqrt, scale=1.0 / D, bias=EPS),
                  reads=[("ss", blk)], writes=[("rstd", blk)])
            S.add("dve", lambda e, blk=blk: e.reciprocal(rstd[:, blk:blk + 1], rstd[:, blk:blk + 1]),
                  reads=[("rstd", blk)], writes=[("rstd", blk)])
            S.add("dve", lambda e, blk=blk: e.scalar_tensor_tensor(mtok[:, blk, :], mtok[:, blk, :], rstd[:, blk:blk + 1], gain_bc,
                                                                   op0=ALU.mult, op1=ALU.mult),
                  reads=[("mtok", blk), ("rstd", blk), "gain_bc"], writes=[("mtok", blk)])
            S.add("dve", lambda e, blk=blk: e.tensor_tensor(h_tok[:, blk, :], h_tok[:, blk, :], mtok[:, blk, :], op=ALU.add),
                  reads=[("mtok", blk), ("h", blk)], writes=[("h", blk)])


    def ffn(layer, halo_only=False):
        rmsnorm_T(ALLH, 1 if layer == 0 else 4, hT, "hT", R_hnb2)
        cols = []
        for g in range(FC // 2):
            cols.append((g * 256, 128, ("g", 2 * g)))
            cols.append((g * 256 + 128, 128, ("g", 2 * g + 1)))
            if not halo_only:
                cols.append((DFF + g * 256, 128, ("u", 2 * g)))
                cols.append((DFF + g * 256 + 128, 128, ("u", 2 * g + 1)))
        slot = [0]
        pend = {}

        def consume(tag, b, cn):
            typ, c = tag
            if typ == "g":
                s = slot[0] % 3
                slot[0] += 1
                S.add("dve", lambda e, s=s, c=c: e.tensor_copy(gpre[:, s, 0:2], halo_g[:, c, :]), reads=[("halog", c)], writes=[("gpre", s)])
                S.add("act", lambda e, s=s, b=b: e.activation(gpre[:, s, 2:514], PS(b), AF.Copy), reads=PSK(b), writes=[("gpre", s)])
                pend[c] = s

                def st_conv():
                    S.add("dve", lambda e: e.tensor_copy(halo_g[:, c, :], gpre[:, s, 512:514]), reads=[("gpre", s)], writes=[("halog", c)])
                    if halo_only:
                        return None
                    S.add("dve", lambda e: e.tensor_scalar(gacc[:, s, :], gpre[:, s, 0:512], fconvw[:, layer, c, 0:1], None, op0=ALU.mult),
                          reads=[("gpre", s), "c_fconvw"], writes=[("gacc", s)])
                    for j in range(1, 3):
                        S.add("dve", lambda e, j=j: e.scalar_tensor_tensor(gacc[:, s, :], gpre[:, s, j:j + 512], fconvw[:, layer, c, j:j + 1], gacc[:, s, :], op0=ALU.mult, op1=ALU.add),
                              reads=[("gpre", s), "c_fconvw", ("gacc", s)], writes=[("gacc", s)])

                    def st_silu():
                        S.add("act", lambda e: e.activation(gacc[:, s, :], gacc[:, s, :], AF.Silu, bias=fconvb[:, layer, c:c + 1]),
                              reads=[("gacc", s), "c_fconvb"], writes=[("gacc", s)])
                        return None
                    return st_silu
                return st_conv
            else:
                s = pend.pop(c)

                def st_w1():
                    def st_mul():
                        S.add("dve", lambda e: e.tensor_tensor(actT[:, c, :], gacc[:, s, :], PS(b), op=ALU.mult),
                              reads=[("gacc", s)] + PSK(b), writes=["actT"])
                        return None
                    return st_mul
                return st_w1
        proj_chunks(w_up[layer], KC, hT, HTK, cols, consume, banks=(0, 1, 2, 3, 4, 5, 6, 7), lag=1)
        S.barrier()
        if halo_only:
            return
        phaseC(w_dn[layer], FC, actT, ["actT"], 1 if layer == 0 else 3)

    def kv_tile(t):
        S.barrier()
        rmsnorm_T(ALLH, 2, hT, "hT", R_hnb2)
        cols = [(c * 128, 128, c) for c in range(32)]
        slot = [0]

        def consume(tag, b, cn):
            c = tag
            s = slot[0] % 3
            bank = slot[0] % 2
            slot[0] += 1
            S.add("act", lambda e, s=s, b=b: e.activation(kvT[:, s, :], PS(b), AF.Copy), reads=PSK(b), writes=[("kvT", s)])
            if c < 16:
                dma("sp", KT_d[c, :, t * TT:(t + 1) * TT], kvT[:, s, :], [("kvT", s)], [("KT_d", c)], "st")
                return None
            cv = c - 16

            def st2():
                def evv(pv, pk, cv=cv):
                    S.add("dve", lambda e: e.tensor_copy(vtok2[:, :, cv * 128:(cv + 1) * 128], pv), reads=pk, writes=["vtok2"])
                transposes_to_tok(kvT[:, s, :], ("kvT", s), evv, None, BF16, bank=bank)
                return None
            return st2
        proj_chunks(w_kv, KC, hT, HTK, cols, consume)
        for blk in range(4):
            dma("sp", V_d[t * TT + blk * 128:t * TT + (blk + 1) * 128, :], vtok2[:, blk, :], ["vtok2"], ["V_d"], "st")

    R.reset()
    KTs_bufs = [R.alloc([2, SEXT], BF16), R.alloc([2, SEXT], BF16)]
    Vs = R.alloc([NKB, 258], BF16)
    qTs = R.alloc([2, TT], BF16)
    PTs = R.alloc([3, TT], BF16)
    om = [R.alloc([4, 256], F32), R.alloc([4, 256], F32)]
    ocomb = R.alloc([4, 256], F32)
    otmp = R.alloc([4, 256], F32)
    attn_tok = R.alloc([4, D], BF16)
    R_hnb3 = R.alloc([2, D], BF16)
    SCALE = 128 ** -0.5
    lamv = smallf[:, 480:484]
    den = smallf[:, 488:496]
    ss4 = smallf[:, 496:500]

    def l1_setup():
        for i in range(2):
            S.add("dve", lambda e, i=i: e.tensor_tensor(lq_b[:, 2 * i, :], lq_b[:, 2 * i, :], lq_b[:, 2 * i + 1, :], op=ALU.mult),
                  reads=[("c_lq", 2 * i), ("c_lq", 2 * i + 1)], writes=[("c_lq", 2 * i)])
            S.add("dve", lambda e, i=i: e.reduce_sum(lamv[:, i:i + 1], lq_b[:, 2 * i, :], axis=AX.X), reads=[("c_lq", 2 * i)], writes=[("lam", i)])
            S.add("act", lambda e, i=i: e.activation(lamv[:, i:i + 1], lamv[:, i:i + 1], AF.Exp), reads=[("lam", i)], writes=[("lam", i)])
        S.add("dve", lambda e: e.tensor_tensor(lamv[:, 2:3], lamv[:, 1:2], lamv[:, 0:1], op=ALU.subtract), reads=[("lam", 0), ("lam", 1)], writes=["nlam"])
        S.add("dve", lambda e: e.tensor_scalar(lamv[:, 2:3], lamv[:, 2:3], -LAMBDA_INIT, None, op0=ALU.add), reads=["nlam"], writes=["nlam"])
        S.add("dve", lambda e: e.tensor_scalar(subln_b, subln_b, 1.0 - LAMBDA_INIT, None, op0=ALU.mult), reads=["c_subln"], writes=["c_subln"])

    def l1_tile(t, halo=False):
        S.barrier()
        nkb = 4 * (t + 1)
        nkeys = nkb * 128
        for blk in range(4):
            r0 = (t - OWN0 + HB) * TT + blk * 128
            dma("sp", h_tok[:, blk, :], h1_d[r0:r0 + 128, :], ["h1_d"], [("h", blk)], "ld")
        rmsnorm_T(ALLH, 3, hT, "hT", R_hnb3)
        S.add("dve", lambda e: e.memset(Vs[:, :, 256:258], 1.0), writes=["Vs"])
        def load_K(hd_):
            kb_ = KTs_bufs[hd_ % 2]
            for m in range(2):
                dma("sp", kb_[:, m, 0:nkeys], KT_d[2 * hd_ + m, :, 0:nkeys], [("KT_d", 2 * hd_ + m)], [("KTs", hd_ % 2)], "ld")
        load_K(0)
        for hd in range(8):
            KTs = KTs_bufs[hd % 2]
            kkey = ("KTs", hd % 2)
            if hd + 1 < 8:
                load_K(hd + 1)
            dma("sp", Vs[:, 0:nkb, 0:256], V_d[0:nkeys, hd * 256:(hd + 1) * 256].rearrange("(kb p) e -> p kb e", p=128), ["V_d"], ["Vs"], "ld")

            def consq(tag, b, cn):
                S.add("act", lambda e, tag=tag, b=b: e.activation(qTs[:, tag, :], PS(b), AF.Copy), reads=PSK(b), writes=[("qTs", tag)])
            proj_chunks(w_q, KC, hT, HTK, [(hd * 256, 128, 0), (hd * 256 + 128, 128, 1)], consq, banks=(2, 3))
            sctr = 0
            for m in range(2):
                prev = None
                for kb in range(nkb + 1):
                    cur = None
                    if kb < nkb:
                        q0 = max(0, kb - 4 * t)
                        qs = slice(q0 * 128, TT)
                        sb_ = 2 + (sctr % 2)
                        sl = sctr % 3
                        sctr += 1
                        S.add("pe", lambda e, m=m, kb=kb, qs=qs, sb_=sb_, KTs=KTs: e.matmul(PS(sb_)[:, qs], lhsT=KTs[:, m, kb * 128:(kb + 1) * 128], rhs=qTs[:, m, qs], start=True, stop=True),
                              reads=[kkey, ("qTs", m)], writes=PSK(sb_))
                        S.add("act", lambda e, kb=kb, qs=qs, sb_=sb_, sl=sl: e.activation(PTs[:, sl, qs], PS(sb_)[:, qs], AF.Exp, scale=SCALE, bias=maskb[:, kb:kb + 1]),
                              reads=PSK(sb_) + ["c_maskb"], writes=[("PT", sl)])
                        if kb >= 4 * t:
                            S.add("dve", lambda e, q0=q0, sl=sl: e.memset(PTs[64:128, sl, q0 * 128:q0 * 128 + 64], 0.0), reads=[("PT", sl)], writes=[("PT", sl)])
                        cur = (kb, q0, sl)
                    if prev is not None:
                        pkb, pq0, psl = prev
                        for qb in range(pq0, 4):
                            def av(e, pkb=pkb, qb=qb, psl=psl, t=t):
                                e.matmul(PS(4 + qb)[:, 0:256], lhsT=PTs[:, psl, qb * 128:(qb + 1) * 128], rhs=Vs[:, pkb, 0:256],
                                         start=(pkb == 0), stop=(pkb == 4 * t + qb))
                                return e.matmul(PS(4 + qb)[:, 256:257], lhsT=PTs[:, psl, qb * 128:(qb + 1) * 128], rhs=ones_b[:, 0:1],
                                                start=False, stop=(pkb == 4 * t + qb), skip_group_check=True)
                            S.add("pe", av, reads=[("PT", psl), "Vs", "c_cmatb1"], writes=PSK(4 + qb))
                    prev = cur
                for qb in range(4):
                    S.add("dve", lambda e, qb=qb, m=m: e.tensor_scalar(den[:, 4 * m + qb:4 * m + qb + 1], PS(4 + qb)[:, 256:257], 1e-30, None, op0=ALU.max),
                          reads=PSK(4 + qb), writes=[("den", m, qb)])
                    S.add("dve", lambda e, qb=qb, m=m: e.reciprocal(den[:, 4 * m + qb:4 * m + qb + 1], den[:, 4 * m + qb:4 * m + qb + 1]),
                          reads=[("den", m, qb)], writes=[("den", m, qb)])
                    S.add("act", lambda e, qb=qb, m=m: e.activation(om[m][:, qb, :], PS(4 + qb)[:, 0:256], AF.Copy, scale=den[:, 4 * m + qb:4 * m + qb + 1]),
                          reads=PSK(4 + qb) + [("den", m, qb)], writes=[("om", m)])
            S.add("dve", lambda e: e.scalar_tensor_tensor(ocomb, om[1], lamv[:, 2:3], om[0], op0=ALU.mult, op1=ALU.add),
                  reads=[("om", 0), ("om", 1), "nlam"], writes=["ocomb"])
            S.add("dve", lambda e: e.tensor_tensor(otmp, ocomb, ocomb, op=ALU.mult), reads=["ocomb"], writes=["otmp"])
            S.add("dve", lambda e: e.reduce_sum(ss4, otmp, axis=AX.X), reads=["otmp"], writes=["ss4"])
            S.add("act", lambda e: e.activation(ss4, ss4, AF.Sqrt, scale=1.0 / 256, bias=EPS), reads=["ss4"], writes=["ss4"])
            S.add("dve", lambda e: e.reciprocal(ss4, ss4), reads=["ss4"], writes=["ss4"])
            S.add("dve", lambda e: e.tensor_tensor(otmp, ocomb, bc(ss4, [128, 4, 256], 2), op=ALU.mult), reads=["ocomb", "ss4"], writes=["otmp"])
            S.add("dve", lambda e, hd=hd: e.tensor_tensor(attn_tok[:, :, hd * 256:(hd + 1) * 256], otmp, bc(subln_b, [128, 4, 256], 1), op=ALU.mult),
                  reads=["otmp", "c_subln"], writes=["attn_tok"])
        for blk in range(4):
            for half in range(2):
                pvo = PS(half).bitcast(BF16).rearrange("p (a b) -> p a b", a=8)

                def tra(e, blk=blk, half=half, pvo=pvo):
                    ins = None
                    for j in range(8):
                        kc = half * 8 + j
                        ins = e.transpose(pvo[:, j, :], attn_tok[:, blk, kc * 128:(kc + 1) * 128], ident_b)
                    return ins
                S.add("pe", tra, reads=["attn_tok", "c_cmatb0"], writes=PSK(half))
                S.add("act", lambda e, blk=blk, half=half, pvo=pvo: e.activation(onT[:, half * 8:half * 8 + 8, blk * 128:(blk + 1) * 128], pvo, AF.Copy),
                      reads=PSK(half), writes=["onT"])
        S.barrier()
        phaseC(w_o, KC, onT, ["onT"], 2)
        ffn(1, halo_only=halo)
        if not halo:
            for blk in range(4):
                r0 = (t - OWN0) * TT + blk * 128
                dma("sp", out[r0:r0 + 128, :], h_tok[:, blk, :], [("h", blk)], ["out"], "st")

    for t in range(NT):
        try:
            layer0_tile(t)
        except StopBuild:
            break
        if do_l1:
            kv_tile(t)
        if do_l1 and t >= OWN0 - HB:
            for blk in range(4):
                r0 = (t - OWN0 + HB) * TT + blk * 128
                dma("sp", h1_d[r0:r0 + 128, :], h_tok[:, blk, :], [("h", blk)], ["h1_d"], "st")
        if t >= OWN0:
            if do_l1:
                pass
            else:
                for blk in range(4):
                    r0 = (t - OWN0) * TT + blk * 128
                    dma("sp", out[r0:r0 + 128, :], h_tok[:, blk, :], [("h", blk)], ["out"], "st")
        S.barrier()

    if do_l1 and stop > 50:
        l1_setup()
        if HB:
            l1_tile(OWN0 - 1, halo=True)
        else:
            S.add("dve", lambda e: e.memset(halo_g, 0.0), reads=[("halog", c) for c in range(FC)], writes=[("halog", c) for c in range(FC)])
        for t in range(OWN0, NT):
            l1_tile(t)
    S.barrier()
    S.add("sp", lambda e: e.dma_start(out=h1_d[0:1, 0:64], in_=x[0:1, 0:64]), reads=[], writes=["fin"], dma="st")
    S.barrier()
    S.add("act", lambda e: e.activation(ss, ss, AF.Copy), reads=["fin"], writes=["fin2"])
    S.emit(nc, es)
    es.close()
    return nc, S


def make_consts():
    ident = np.eye(128, dtype=np.float32)
    j = np.arange(128)[:, None]
    i = np.arange(128)[None, :]
    same = (j // 64) == (i // 64)
    NEGm = np.where(same & (i >= j), 0.0, NEG).astype(np.float32)
    STRm = (same & (i > j)).astype(np.float32)
    TRI_T = (same & (j <= i)).astype(np.float32)
    BLK = same.astype(np.float32)
    ones = np.ones((128, 128), np.float32)
    return np.stack([ident, NEGm, STRm, TRI_T, BLK, ones], 0)


def host_inputs(inputs, NT=8, OWN0=4):
    f = lambda a: np.ascontiguousarray(np.asarray(a, dtype=np.float32))
    x = f(inputs["x"])
    B = x.shape[0]
    fm = lambda v: np.ascontiguousarray(np.asarray(v, np.float32).reshape(16, 128).T)
    gp = np.concatenate([fm(inputs["norm_mix_pre"][0]), fm(inputs["norm_ffn_pre"][0]), fm(inputs["kv_norm_w"]),
                         fm(inputs["norm_mix_pre"][1]), fm(inputs["norm_ffn_pre"][1])], axis=1)
    gpost = np.stack([f(inputs["norm_mix_post"][0]), f(inputs["norm_ffn_post"][0]), f(inputs["norm_mix_post"][1]), f(inputs["norm_ffn_post"][1])], 0)
    cw = f(inputs["gdn_conv_w"][0])
    convw = np.ascontiguousarray(cw.T.reshape(48, 128, 4).transpose(1, 0, 2).reshape(128, 48 * 4))
    fw = f(inputs["ffn_conv_w"])
    fconvw = np.ascontiguousarray(fw.transpose(0, 2, 1).reshape(2, FC, 128, 3).transpose(2, 0, 1, 3).reshape(128, 2 * FC * 3))
    fb = f(inputs["ffn_conv_b"])
    fconvb = np.ascontiguousarray(fb.reshape(2, FC, 128).transpose(2, 0, 1).reshape(128, 2 * FC))
    lq = np.stack([f(inputs["diff_lq1"][0]), f(inputs["diff_lk1"][0]), f(inputs["diff_lq2"][0]), f(inputs["diff_lk2"][0])], 0)
    common = {
        "w_in": f(inputs["gdn_w_in"][0]), "w_out": f(inputs["gdn_w_out"][0]), "w_kv": f(inputs["w_kv"]),
        "w_q": f(inputs["diff_w_q"][0]), "w_o": f(inputs["diff_w_o"][0]),
        "w_up0": f(inputs["ffn_w_up"][0]), "w_up1": f(inputs["ffn_w_up"][1]),
        "w_dn0": f(inputs["ffn_w_down"][0]), "w_dn1": f(inputs["ffn_w_down"][1]),
        "gp": gp, "gpost": gpost, "convw": convw, "fconvw": fconvw, "fconvb": fconvb,
        "alog": f(inputs["gdn_a_log"][0]), "dtb": f(inputs["gdn_dt_bias"][0]), "normw": f(inputs["gdn_norm_w"][0]),
        "subln": f(inputs["diff_subln_w"][0]), "lq": lq, "cmat": make_consts(),
    }
    SEXT = NT * TT
    SOWN = (NT - OWN0) * TT
    maps = []
    for c in range(8):
        b, half = c // 2, c % 2
        xe = np.zeros((SEXT, D), np.float32)
        mask = np.zeros((SEXT,), np.float32)
        if half == 1:
            xe[:] = x[b, 0:SEXT]
        else:
            xe[SEXT - SOWN:] = x[b, 0:SOWN]
            mask[:SEXT - SOWN] = NEG
        m = dict(common)
        m["x"] = xe
        mk = np.zeros((128, 32), np.float32)
        mk[:, :SEXT // 128] = mask.reshape(SEXT // 128, 128).T
        m["maskb"] = mk
        maps.append(m)
    return maps


_CACHE = {}


def kernel(**inputs):
    NT, OWN0 = 8, 4
    if "nc" not in _CACHE:
        _CACHE["nc"] = build_program(NT, OWN0, debug=False, do_l1=True)[0]
    nc = _CACHE["nc"]
    maps = host_inputs(inputs, NT, OWN0)
    res = run_bass_kernel_spmd(nc, maps, core_ids=list(range(8)))
    x = np.asarray(inputs["x"])
    B, Sq, _ = x.shape
    outp = np.zeros((B, Sq, D), np.float32)
    for c in range(8):
        b, half = c // 2, c % 2
        outp[b, half * 2048:(half + 1) * 2048] = res.results[c]["out"]
    return outp
```

```python
import numpy as np
import ml_dtypes
from contextlib import ExitStack
import concourse.bass as bass
import concourse.mybir as mybir
from concourse.bass_utils import run_bass_kernel_spmd

F32 = mybir.dt.float32
BF16 = mybir.dt.bfloat16
AF = mybir.ActivationFunctionType
ALU = mybir.AluOpType
AX = mybir.AxisListType

D = 2048
KC = 16
TT = 512
H = 16
DFF = 5632
FC = 44
EPS = 1e-6
NEG = -30000.0
LAMBDA_INIT = 0.8 - 0.6 * float(np.exp(-0.3 * 1))


class Op:
    __slots__ = ("eng", "fn", "deps", "dma", "sig", "sigval", "semkey", "order")


class Sched:
    ENGS = ("pe", "act", "dve", "pool", "sp")

    def __init__(self):
        self.ops = []
        self.by_eng = {e: [] for e in self.ENGS}
        self.lastw = {}
        self.rd = {}
        self.bar = None
        self.absorbed = set()
        self.dma_since = []
        self.stream_ops = {}
        self.RING = 8

    def add(self, eng, fn, reads=(), writes=(), dma=None, nobar=False):
        op = Op()
        op.eng, op.fn, op.dma, op.sig, op.sigval, op.semkey = eng, fn, dma, False, 0, None
        op.order = len(self.ops)
        deps = set()
        for k in reads:
            w = self.lastw.get(k)
            if w is not None:
                deps.add(w)
        for k in writes:
            w = self.lastw.get(k)
            if w is not None:
                deps.add(w)
            for r in self.rd.get(k, {}).values():
                deps.add(r)
        if dma is not None:
            so = self.stream_ops.setdefault(dma, [])
            if len(so) >= self.RING:
                deps.add(so[-self.RING])
            so.append(op)
        if self.bar is not None and not nobar and eng not in self.absorbed:
            deps |= self.bar
            self.absorbed.add(eng)
        for k in writes:
            self.lastw[k] = op
            self.rd[k] = {}
        rk = eng if dma is None else ("dma", op.order)
        for k in reads:
            self.rd.setdefault(k, {})[rk] = op
        op.deps = deps
        self.ops.append(op)
        self.by_eng[eng].append(op)
        if dma is not None:
            self.dma_since.append(op)
        return op

    def barrier(self):
        last = set()
        for e in self.ENGS:
            for op in reversed(self.by_eng[e]):
                if op.dma is None:
                    last.add(op)
                    break
        last |= set(self.dma_since)
        self.dma_since = []
        self.bar = last
        self.absorbed = set()

    def emit(self, nc, es):
        for op in self.ops:
            for d in op.deps:
                if d.dma is not None or op.dma is not None or d.eng != op.eng or op.eng != "pe":
                    d.sig = True
        cnt = {}
        scount = {}
        for op in self.ops:
            if op.dma is not None:
                i = scount.get(op.dma, 0)
                scount[op.dma] = i + 1
                s = ("dma", op.dma, i % self.RING)
                cnt[s] = cnt.get(s, 0) + 16
                op.sigval, op.semkey = cnt[s], s
            elif op.sig:
                s = ("eng", op.eng)
                cnt[s] = cnt.get(s, 0) + 1
                op.sigval, op.semkey = cnt[s], s
        sems = {}
        for i, s in enumerate(sorted(cnt.keys(), key=str)):
            sems[s] = es.enter_context(nc.semaphore(f"sem{i}_" + "_".join(str(x) for x in s[1:])))
        self.maxcnt = dict(cnt)
        block = es.enter_context(nc.Block())

        def mk(engname):
            def body(e):
                waited = {}
                for op in self.by_eng[engname]:
                    need = {}
                    for d in op.deps:
                        if d.dma is None and op.dma is None and d.eng == op.eng and op.eng == "pe":
                            continue
                        if need.get(d.semkey, 0) < d.sigval:
                            need[d.semkey] = d.sigval
                    for s, v in need.items():
                        if waited.get(s, 0) < v:
                            e.wait_ge(sems[s], v)
                            waited[s] = v
                    ins = op.fn(e)
                    if op.dma is not None:
                        ins.then_inc(sems[op.semkey], 16)
                    elif op.sig:
                        ins.then_inc(sems[op.semkey], 1)
            return body

        block.tensor(mk("pe"))
        block.scalar(mk("act"))
        block.vector(mk("dve"))
        block.gpsimd(mk("pool"))
        block.sync(mk("sp"))


class Carver:
    def __init__(self, ap2d_bf16, nelem):
        self.t = ap2d_bf16
        self.n = nelem
        self.off = 0

    def reset(self, off=0):
        self.off = off

    def alloc(self, shape, dt):
        n = 1
        for s in shape:
            n *= s
        ne = n * 2 if dt == F32 else n
        ne = (ne + 1) // 2 * 2
        assert self.off + ne <= self.n, f"carver overflow {self.off}+{ne}>{self.n}"
        v = self.t[:, self.off:self.off + ne]
        self.off += ne
        if dt == F32:
            v = v.bitcast(F32)
        if len(shape) == 2:
            v = v.rearrange("p (a b) -> p a b", a=shape[0])
        elif len(shape) == 3:
            v = v.rearrange("p (a b c) -> p a b c", a=shape[0], b=shape[1])
        return v


def bc(ap, shape, axis):
    return ap.unsqueeze(axis).to_broadcast(list(shape))


class StopBuild(Exception):
    pass


def build_program(NT=8, OWN0=4, debug=False, do_l1=True, stop=99):
    nc = bass.Bass("TRN2", target_bir_lowering=False)
    SEXT = NT * TT
    NOWN = NT - OWN0
    SOWN = NOWN * TT
    NKB = SEXT // 128

    def din(name, shape, dt=F32):
        return nc.dram_tensor(name, list(shape), dt, kind="ExternalInput").ap()

    x = din("x", [SEXT, D])
    w_in = din("w_in", [D, 8224])
    w_out = din("w_out", [D, D])
    w_kv = din("w_kv", [D, 4096])
    w_q = din("w_q", [D, D])
    w_o = din("w_o", [D, D])
    w_up = [din("w_up0", [D, 2 * DFF]), din("w_up1", [D, 2 * DFF])]
    w_dn = [din("w_dn0", [DFF, D]), din("w_dn1", [DFF, D])]
    gp_d = din("gp", [128, 5 * 16])
    gpost_d = din("gpost", [4, D])
    convw_d = din("convw", [128, 48 * 4])
    fconvw_d = din("fconvw", [128, 2 * FC * 3])
    fconvb_d = din("fconvb", [128, 2 * FC])
    alog_d = din("alog", [16])
    dtb_d = din("dtb", [16])
    normw_d = din("normw", [128])
    subln_d = din("subln", [256])
    lq_d = din("lq", [4, 128])
    cmat_d = din("cmat", [6, 128, 128])
    maskb_d = din("maskb", [128, 32])
    out = nc.dram_tensor("out", [SOWN, D], F32, kind="ExternalOutput").ap()
    dbg = {}
    if debug:
        for nm, shp, dt in (("d_qT", [128, 8 * TT], BF16), ("d_kT", [128, 8 * TT], BF16), ("d_vtok", [128, 4 * 1024], BF16),
                            ("d_onT", [128, KC * TT], BF16), ("d_hmix", [128, 4 * D], F32), ("d_bg", [128, 4 * 16 * 3], F32),
                            ("d_oblk", [128, 1024], F32), ("d_TT", [128, 1024], BF16), ("d_AT", [128, 1024], BF16)):
            dbg[nm] = nc.dram_tensor(nm, shp, dt, kind="ExternalOutput").ap()
    HB = 1 if OWN0 > 0 else 0
    h1_d = nc.dram_tensor("h1_scr", [SOWN + HB * TT, D], F32).ap()
    KT_d = nc.dram_tensor("KT_scr", [16, 128, SEXT], BF16).ap()
    V_d = nc.dram_tensor("V_scr", [SEXT, D], BF16).ap()

    es = ExitStack()
    S = Sched()

    def sb(name, shape, dt):
        return es.enter_context(nc.sbuf_tensor("s_" + name, list(shape), dt))[:]

    cmat = sb("cmat", [128, 6, 128], F32)
    ident_f, NEGm, STRm, TRI_T, BLKm, ones_f = (cmat[:, i, :] for i in range(6))
    cmat_b = sb("cmat_b", [128, 2, 128], BF16)
    ident_b, ones_b = cmat_b[:, 0, :], cmat_b[:, 1, :]
    gp = sb("gp", [128, 5, 16], F32)
    convw = sb("convw", [128, 48, 4], F32)
    fconvw = sb("fconvw", [128, 2, FC, 3], F32)
    fconvb = sb("fconvb", [128, 2, FC], F32)
    alog_b = sb("alog_b", [128, 16], F32)
    dtb_b = sb("dtb_b", [128, 16], F32)
    normw_b = sb("normw_b", [128, 128], F32)
    subln_b = sb("subln_b", [128, 256], F32)
    lq_b = sb("lq_b", [128, 4, 128], F32)
    maskb = sb("maskb", [128, 32], F32)
    halo_qkv = sb("halo_qkv", [128, 48, 3], F32)
    halo_g = sb("halo_g", [128, FC, 2], F32)
    Sst = sb("Sst", [128, 16, 128], F32)
    Sbf = sb("Sbf", [128, 16, 128], BF16)
    wsl = [sb("wsl0", [128, 4096], BF16), sb("wsl1", [128, 4096], BF16)]
    wba = sb("wba", [128, 16, 32], BF16)
    hT = sb("hT", [128, KC, TT], BF16)
    onT = sb("onT", [128, KC, TT], BF16)
    h_tok = sb("h_tok", [128, 4, D], F32)
    junk_a = sb("junk_a", [128, D], BF16)
    smallf = sb("smallf", [128, 640], F32)
    ss = smallf[:, 0:4]
    rstd = smallf[:, 4:8]
    ssp = smallf[:, 8:24].rearrange("p (a b) -> p a b", a=4)
    beta = smallf[:, 32:96].rearrange("p (a b) -> p a b", a=4)
    gg = smallf[:, 96:160].rearrange("p (a b) -> p a b", a=4)
    gcs = smallf[:, 160:224].rearrange("p (a b) -> p a b", a=4)
    nbeta = smallf[:, 224:288].rearrange("p (a b) -> p a b", a=4)
    ekl = smallf[:, 288:352].rearrange("p (a b) -> p a b", a=4)
    negA = smallf[:, 352:368]
    ssh = smallf[:, 368:384]
    lam_t = smallf[:, 384:392]
    tmp16 = smallf[:, 400:464].rearrange("p (a b) -> p a b", a=4)
    RN = 49300
    Rt = sb("Rregion", [128, RN], BF16)
    R = Carver(Rt, RN)

    psum_t = es.enter_context(nc.psum_tensor("psum_all", [128, 8 * 512], F32))[:]

    def PS(b, nb=1):
        return psum_t[:, b * 512:(b + nb) * 512]

    def PSK(b, nb=1):
        return [("ps", i) for i in range(b, b + nb)]

    def dma(eng, out_ap, in_ap, reads, writes, stream, nobar=False):
        return S.add(eng, lambda e: e.dma_start(out=out_ap, in_=in_ap), reads=reads, writes=writes, dma=stream, nobar=nobar)

    dma("sp", cmat, cmat_d.rearrange("c p n -> p c n"), [], ["c_cmat"], "ld")
    dma("pool", cmat_b[:, 0, :], cmat_d[0], [], ["c_cmatb0"], "pl")
    dma("pool", cmat_b[:, 1, :], cmat_d[5], [], ["c_cmatb1"], "pl")
    dma("sp", gp, gp_d.rearrange("p (a b) -> p a b", a=5), [], ["c_gp"], "ld")
    dma("sp", convw, convw_d.rearrange("p (a b) -> p a b", a=48), [], ["c_convw"], "ld")
    dma("sp", fconvw, fconvw_d.rearrange("p (l a b) -> p l a b", l=2, a=FC), [], ["c_fconvw"], "ld")
    dma("sp", fconvb, fconvb_d.rearrange("p (l a) -> p l a", l=2), [], ["c_fconvb"], "ld")
    dma("sp", alog_b, alog_d.partition_broadcast(128), [], ["c_alog"], "ld")
    dma("sp", dtb_b, dtb_d.partition_broadcast(128), [], ["c_dtb"], "ld")
    dma("sp", normw_b, normw_d.partition_broadcast(128), [], ["c_normw"], "ld")
    dma("sp", subln_b, subln_d.partition_broadcast(128), [], ["c_subln"], "ld")
    for i in range(4):
        dma("sp", lq_b[:, i, :], lq_d[i].partition_broadcast(128), [], [("c_lq", i)], "ld")
    dma("sp", maskb, maskb_d, [], ["c_maskb"], "ld")
    dma("pool", wba, w_in[:, 8192:8224].rearrange("(k p) n -> p k n", p=128), [], ["c_wba"], "pl")
    CONST = ["c_cmat", "c_cmatb0", "c_cmatb1"]
    S.add("dve", lambda e: e.memset(halo_qkv, 0.0), writes=[("halo", c) for c in range(48)])
    S.add("dve", lambda e: e.memset(halo_g, 0.0), writes=[("halog", c) for c in range(FC)])
    S.add("dve", lambda e: e.memset(Sst, 0.0), writes=["Sst"])
    S.add("dve", lambda e: e.memset(Sbf, 0.0), writes=["Sbf"])
    S.add("act", lambda e: e.activation(negA, alog_b, AF.Exp), reads=["c_alog"], writes=["negA"])
    S.add("dve", lambda e: e.tensor_scalar(negA, negA, -1.0, None, op0=ALU.mult), reads=["negA"], writes=["negA"])

    wctr = [0]

    def load_w(wsrc, k0, nk, c0, ncols):
        s = wctr[0] % 2
        wctr[0] += 1
        view = wsl[s][:, 0:nk * ncols].rearrange("p (k n) -> p k n", k=nk)
        src = wsrc[k0 * 128:(k0 + nk) * 128, c0:c0 + ncols].rearrange("(k p) n -> p k n", p=128)
        dma("pool", view, src, [], [("wsl", s)], f"w{s}", nobar=True)
        return view, ("wsl", s)

    def rmsnorm_T(hkeys, gidx, dstT, dkey, hnb, ps_banks=(0, 1)):
        for blk in range(4):
            S.add("act", lambda e, blk=blk: e.activation(junk_a, h_tok[:, blk, :], AF.Square, accum_out=ss[:, blk:blk + 1]),
                  reads=[hkeys[blk]], writes=[("ss", blk)])
            S.add("act", lambda e, blk=blk: e.activation(rstd[:, blk:blk + 1], ss[:, blk:blk + 1], AF.Sqrt, scale=1.0 / D, bias=EPS),
                  reads=[("ss", blk)], writes=[("rstd", blk)])
            S.add("dve", lambda e, blk=blk: e.reciprocal(rstd[:, blk:blk + 1], rstd[:, blk:blk + 1]),
                  reads=[("rstd", blk)], writes=[("rstd", blk)])
            sl = blk % 2
            S.add("dve", lambda e, blk=blk, sl=sl: e.tensor_scalar(hnb[:, sl, :], h_tok[:, blk, :], rstd[:, blk:blk + 1], None, op0=ALU.mult),
                  reads=[hkeys[blk], ("rstd", blk)], writes=[("hnb", sl)])
            for half in range(2):
                b = ps_banks[half]
                pv = PS(b).bitcast(BF16).rearrange("p (a b) -> p a b", a=8)

                def tr(e, half=half, sl=sl, pv=pv):
                    ins = None
                    for j in range(8):
                        kc = half * 8 + j
                        ins = e.transpose(pv[:, j, :], hnb[:, sl, kc * 128:(kc + 1) * 128], ident_b)
                    return ins
                S.add("pe", tr, reads=[("hnb", sl), "c_cmatb0"], writes=PSK(b))
                for j in range(8):
                    kc = half * 8 + j
                    dst = dstT[:, kc, blk * 128:(blk + 1) * 128]
                    if half == 0:
                        S.add("act", lambda e, dst=dst, pv=pv, j=j, kc=kc: e.activation(dst, pv[:, j, :], AF.Copy, scale=gp[:, gidx, kc:kc + 1]),
                              reads=PSK(b) + ["c_gp"], writes=[(dkey, blk, half)])
                    else:
                        S.add("dve", lambda e, dst=dst, pv=pv, j=j, kc=kc: e.tensor_scalar(dst, pv[:, j, :], gp[:, gidx, kc:kc + 1], None, op0=ALU.mult),
                              reads=PSK(b) + ["c_gp"], writes=[(dkey, blk, half)])

    pctr = [0]

    def proj_chunks(wsrc, nk, actT, akeys, chunk_cols, consume, gcols=256, banks=(2, 3, 4, 5), lag=2):
        parts = [(k0, min(16, nk - k0)) for k0 in range(0, nk, 16)]
        pending = []

        def tick():
            for it in pending:
                it[0] -= 1
            ready = [it for it in pending if it[0] <= 0]
            pending[:] = [it for it in pending if it[0] > 0]
            for _, fn in ready:
                nxt = fn()
                if nxt is not None:
                    pending.append([lag, nxt])

        i = 0
        while i < len(chunk_cols):
            c0 = chunk_cols[i][0]
            run, tot = [], 0
            for cc in chunk_cols[i:]:
                if cc[0] == c0 + tot and tot + cc[1] <= gcols:
                    run.append(cc)
                    tot += cc[1]
                else:
                    break
            bks = []
            for _ in run:
                bks.append(banks[pctr[0] % len(banks)])
                pctr[0] += 1
            for pi, (k0, nkp) in enumerate(parts):
                wv, wkey = load_w(wsrc, k0, nkp, c0, tot)
                off = 0
                for ri, (cc0, cn, tag) in enumerate(run):
                    b = bks[ri]

                    def mm(e, wv=wv, off=off, cn=cn, b=b, k0=k0, nkp=nkp, pi=pi):
                        ins = None
                        for kc in range(nkp):
                            ins = e.matmul(PS(b)[0:cn, :], lhsT=wv[:, kc, off:off + cn], rhs=actT[:, k0 + kc, :],
                                           start=(pi == 0 and kc == 0), stop=(pi == len(parts) - 1 and kc == nkp - 1))
                        return ins
                    S.add("pe", mm, reads=[wkey] + list(akeys), writes=PSK(b))
                    off += cn
                    if pi == len(parts) - 1:
                        nxt = consume(tag, b, cn)
                        if nxt is not None:
                            pending.append([lag + 1, nxt])
                    tick()
            i += len(run)
        while pending:
            _, fn = pending.pop(0)
            nxt = fn()
            if nxt is not None:
                pending.append([0, nxt])

    def transposes_pe(srcT_ap, skey, dt, bank):
        if dt == BF16:
            pv = PS(bank).bitcast(BF16)[:, 0:512].rearrange("p (a b) -> p a b", a=4)
            idn, ik = ident_b, "c_cmatb0"
        else:
            pv = PS(bank).rearrange("p (a b) -> p a b", a=4)
            idn, ik = ident_f, "c_cmat"

        def tr(e):
            ins = None
            for blk in range(4):
                ins = e.transpose(pv[:, blk, :], srcT_ap[:, blk * 128:(blk + 1) * 128], idn)
            return ins
        S.add("pe", tr, reads=[skey, ik], writes=PSK(bank))
        return pv, PSK(bank)

    def transposes_to_tok(srcT_ap, skey, dst_fn, dkeys, dt, bank, eng_evac="act"):
        if dt == BF16:
            pv = PS(bank).bitcast(BF16)[:, 0:512].rearrange("p (a b) -> p a b", a=4)
            idn, ik = ident_b, "c_cmatb0"
        else:
            pv = PS(bank).rearrange("p (a b) -> p a b", a=4)
            idn, ik = ident_f, "c_cmat"

        def tr(e):
            ins = None
            for blk in range(4):
                ins = e.transpose(pv[:, blk, :], srcT_ap[:, blk * 128:(blk + 1) * 128], idn)
            return ins
        S.add("pe", tr, reads=[skey, ik], writes=PSK(bank))
        dst_fn(pv, PSK(bank))

    R.reset()
    qT = R.alloc([8, TT], BF16)
    kT = R.alloc([8, TT], BF16)
    vtok = R.alloc([4, 1024], BF16)
    ktok = R.alloc([4, 1024], BF16)
    zstok = R.alloc([4, 1024], BF16)
    base_AB = R.off
    R_hnb = R.alloc([2, D], BF16)
    pc = R.alloc([3, 516], F32)
    acc = R.alloc([6, TT], F32)
    sq = R.alloc([3, TT], BF16)
    rt = R.alloc([3, TT], F32)
    vzT = R.alloc([3, TT], BF16)
    zT = R.alloc([3, TT], BF16)
    R.reset(base_AB)
    rhs_d = R.alloc([8, 128], F32)
    DT = R.alloc([8, 128], F32)
    EG = R.alloc([8, 128], F32)
    AT = R.alloc([8, 128], BF16)
    Zf = R.alloc([8, 128], F32)
    Qf = R.alloc([8, 128], F32)
    Qb = R.alloc([8, 128], BF16)
    Xb = [R.alloc([8, 128], BF16), R.alloc([8, 128], BF16)]
    Wb = [R.alloc([8, 128], BF16), R.alloc([8, 128], BF16)]
    kdT = R.alloc([8, 128], BF16)
    qdT = R.alloc([8, 128], BF16)
    kdec = R.alloc([8, 128], BF16)
    rr = R.alloc([8, 128], BF16)
    vnew = R.alloc([8, 128], BF16)
    o_blk = R.alloc([8, 128], F32)
    tmpf = R.alloc([8, 128], F32)
    on_tok = R.alloc([8, 128], BF16)
    R.reset()
    mtok = R.alloc([4, D], F32)
    gain_bc = R.alloc([D], F32)
    base_D = R.off
    R_hnb2 = R.alloc([2, D], BF16)
    vtok2 = R.alloc([4, D], BF16)
    R.reset(base_D)
    actT = R.alloc([FC, TT], BF16)
    base_T = R.off
    gpre = R.alloc([3, 516], F32)
    gacc = R.alloc([3, TT], F32)
    R.reset(base_T)
    mT = R.alloc([3, TT], F32)
    kvT = R.alloc([3, TT], BF16)

    ALLH = [("h", b) for b in range(4)]

    def layer0_tile(t):
        for blk in range(4):
            dma("sp", h_tok[:, blk, :], x[t * TT + blk * 128:t * TT + (blk + 1) * 128, :], [], [("h", blk)], "ld")
        if stop <= 0:
            raise StopBuild()
        rmsnorm_T(ALLH, 0, hT, "hT", R_hnb)
        if stop <= 1:
            raise StopBuild()
        for blk in range(4):
            def mmba(e, blk=blk):
                ins = None
                for kc in range(KC):
                    ins = e.matmul(PS(6)[:, blk * 32:(blk + 1) * 32], lhsT=hT[:, kc, blk * 128:(blk + 1) * 128], rhs=wba[:, kc, :],
                                   start=(kc == 0), stop=(kc == KC - 1))
                return ins
            S.add("pe", mmba, reads=HTK + ["c_wba"], writes=PSK(6))
        pba = PS(6)[:, 0:128].rearrange("p (a b) -> p a b", a=4)
        S.add("act", lambda e: e.activation(beta, pba[:, :, 0:16], AF.Sigmoid), reads=PSK(6), writes=["beta"])
        S.add("dve", lambda e: e.tensor_tensor(tmp16, pba[:, :, 16:32], bc(dtb_b, [128, 4, 16], 1), op=ALU.add),
              reads=PSK(6) + ["c_dtb"], writes=["tmp16"])
        S.add("act", lambda e: e.activation(tmp16, tmp16, AF.Exp), reads=["tmp16"], writes=["tmp16"])
        S.add("act", lambda e: e.activation(tmp16, tmp16, AF.Ln, bias=1.0), reads=["tmp16"], writes=["tmp16"])
        S.add("dve", lambda e: e.tensor_tensor(gg, tmp16, bc(negA, [128, 4, 16], 1), op=ALU.mult),
              reads=["tmp16", "negA"], writes=["gg"])
        S.add("dve", lambda e: e.tensor_scalar(nbeta, beta, -1.0, None, op0=ALU.mult), reads=["beta"], writes=["nbeta"])
        for blk in range(4):
            S.add("pe", lambda e, blk=blk: e.matmul(PS(7)[:, blk * 32:blk * 32 + 16], lhsT=TRI_T, rhs=gg[:, blk, :], start=True, stop=True),
                  reads=["gg", "c_cmat"], writes=PSK(7))
            S.add("pe", lambda e, blk=blk: e.matmul(PS(7)[:, blk * 32 + 16:blk * 32 + 32], lhsT=BLKm, rhs=gg[:, blk, :], start=True, stop=True),
                  reads=["gg", "c_cmat"], writes=PSK(7))
        pg = PS(7)[:, 0:128].rearrange("p (a b) -> p a b", a=4)
        S.add("dve", lambda e: e.tensor_copy(gcs, pg[:, :, 0:16]), reads=PSK(7), writes=["gcs"])
        S.add("dve", lambda e: e.tensor_tensor(ekl, pg[:, :, 16:32], gcs, op=ALU.subtract), reads=PSK(7) + ["gcs"], writes=["ekl"])
        S.add("act", lambda e: e.activation(ekl, ekl, AF.Exp), reads=["ekl"], writes=["ekl"])
        if debug and t == 0:
            pass

        if stop <= 2:
            raise StopBuild()
        for hb in range(2):
            S.barrier()
            phaseA(t, hb)
            if stop <= 3:
                raise StopBuild()
            S.barrier()
            phaseB(t, hb)
            if stop <= 4:
                raise StopBuild()
        if stop <= 4.5:
            raise StopBuild()
        S.barrier()
        if debug and t == 0:
            dma("sp", dbg["d_onT"], onT.rearrange("p a b -> p (a b)"), ["onT"], ["d_onT"], "st")
        if stop <= 4.7:
            raise StopBuild()
        phaseC(w_out, KC, onT, ["onT"], 0)
        if stop <= 5:
            raise StopBuild()
        if debug and t == 0:
            dma("sp", dbg["d_hmix"], h_tok.rearrange("p a b -> p (a b)"), ALLH, ["d_hmix"], "st")
        ffn(0)

    HTK = [("hT", b_, h_) for b_ in range(4) for h_ in range(2)]

    def phaseA(t, hb):
        cols = []
        for typ, base in (("q", 0), ("k", 2048), ("v", 4096), ("z", 6144)):
            for h8 in range(8):
                cols.append((base + (8 * hb + h8) * 128, 128, (typ, h8)))
        slot = [0]

        tbank = [0]

        def next_tbank():
            tbank[0] += 1
            return tbank[0] % 2

        def consume(tag, b, cn):
            typ, h8 = tag
            hd = 8 * hb + h8
            n = slot[0]
            slot[0] += 1
            s = n % 3
            sa = n % 6
            if typ == "z":
                S.add("act", lambda e, s=s, b=b: e.activation(zT[:, s, :], PS(b), AF.Silu), reads=PSK(b), writes=[("zT", s)])

                def st_tr():
                    pv, pk = transposes_pe(zT[:, s, :], ("zT", s), BF16, next_tbank())
                    S.add("dve", lambda e: e.tensor_copy(zstok[:, :, h8 * 128:(h8 + 1) * 128], pv), reads=pk, writes=["zstok"])
                    return None
                return st_tr
            c = {"q": 0, "k": 16, "v": 32}[typ] + hd
            S.add("dve", lambda e, s=s, c=c: e.tensor_copy(pc[:, s, 0:3], halo_qkv[:, c, :]), reads=[("halo", c)], writes=[("pc", s)])
            S.add("act", lambda e, s=s, b=b: e.activation(pc[:, s, 3:515], PS(b), AF.Copy), reads=PSK(b), writes=[("pc", s)])
            S.add("act", lambda e, b=b: e.activation(acc[:, sa, :], PS(b), AF.Copy, scale=convw[:, c, 3:4]),
                  reads=PSK(b) + ["c_convw"], writes=[("acc", sa)])

            def st_conv():
                S.add("dve", lambda e, s=s, c=c: e.tensor_copy(halo_qkv[:, c, :], pc[:, s, 512:515]), reads=[("pc", s)], writes=[("halo", c)])
                for j in range(0, 3):
                    S.add("dve", lambda e, j=j: e.scalar_tensor_tensor(acc[:, sa, :], pc[:, s, j:j + 512], convw[:, c, j:j + 1], acc[:, sa, :], op0=ALU.mult, op1=ALU.add),
                          reads=[("pc", s), "c_convw", ("acc", sa)], writes=[("acc", sa)])
                return st_silu

            def st_silu():
                if typ == "v":
                    S.add("act", lambda e: e.activation(vzT[:, s, :], acc[:, sa, :], AF.Silu), reads=[("acc", sa)], writes=[("vzT", s)])

                    def st_trv():
                        pv, pk = transposes_pe(vzT[:, s, :], ("vzT", s), BF16, next_tbank())
                        S.add("dve", lambda e: e.tensor_copy(vtok[:, :, h8 * 128:(h8 + 1) * 128], pv), reads=pk, writes=["vtok"])
                        return None
                    return st_trv
                S.add("act", lambda e: e.activation(acc[:, sa, :], acc[:, sa, :], AF.Silu), reads=[("acc", sa)], writes=[("acc", sa)])
                S.add("act", lambda e: e.activation(sq[:, s, :], acc[:, sa, :], AF.Square), reads=[("acc", sa)], writes=[("sq", s)])
                return st_ones

            dst = (qT if typ == "q" else kT)[:, h8, :]
            dk = ("qT" if typ == "q" else "kT", h8)
            b2box = [None]

            def st_ones():
                b2 = 6 + next_tbank()
                b2box[0] = b2
                S.add("pe", lambda e: e.matmul(PS(b2), lhsT=ones_b, rhs=sq[:, s, :], start=True, stop=True),
                      reads=[("sq", s), "c_cmatb1"], writes=PSK(b2))
                return st_sqrt()

            def st_sqrt():
                b2 = b2box[0]
                if typ == "q":
                    S.add("act", lambda e: e.activation(rt[:, s, :], PS(b2), AF.Sqrt, scale=128.0, bias=128.0 * EPS), reads=PSK(b2), writes=[("rt", s)])
                else:
                    S.add("act", lambda e: e.activation(rt[:, s, :], PS(b2), AF.Sqrt, bias=EPS), reads=PSK(b2), writes=[("rt", s)])
                return st_norm

            def st_norm():
                S.add("dve", lambda e: e.reciprocal(rt[:, s, :], rt[:, s, :]), reads=[("rt", s)], writes=[("rt", s)])
                S.add("dve", lambda e: e.tensor_tensor(dst, acc[:, sa, :], rt[:, s, :], op=ALU.mult), reads=[("acc", sa), ("rt", s)], writes=[dk])
                if typ != "k":
                    return None

                def st_trk():
                    pv, pk = transposes_pe(kT[:, h8, :], dk, BF16, next_tbank())
                    S.add("act", lambda e: e.activation(ktok[:, :, h8 * 128:(h8 + 1) * 128], pv, AF.Copy), reads=pk, writes=["ktok"])
                    return None
                return st_trk
            return st_conv
        proj_chunks(w_in, KC, hT, HTK, cols, consume, lag=1)
        if debug and t == 0 and hb == 0:
            dma("sp", dbg["d_qT"], qT.rearrange("p a b -> p (a b)"), [("qT", i) for i in range(8)], ["d_qT"], "st")
            dma("sp", dbg["d_kT"], kT.rearrange("p a b -> p (a b)"), [("kT", i) for i in range(8)], ["d_kT"], "st")
            dma("sp", dbg["d_vtok"], vtok.rearrange("p a b -> p (a b)"), ["vtok"], ["d_vtok"], "st")
            dma("sp", dbg["d_bg"][:, 0:64], beta.rearrange("p a b -> p (a b)"), ["beta"], ["d_bg0"], "st")
            dma("sp", dbg["d_bg"][:, 64:128], gg.rearrange("p a b -> p (a b)"), ["gg"], ["d_bg1"], "st")
            dma("sp", dbg["d_bg"][:, 128:192], gcs.rearrange("p a b -> p (a b)"), ["gcs"], ["d_bg2"], "st")

    def phaseB(t, hb):
        QK = [("qT", i) for i in range(8)] + [("kT", i) for i in range(8)]
        h0 = 8 * hb
        r2 = lambda ap: ap.rearrange("p a b -> p (a b)")
        if t == 0 and hb == 0:
            pass
        S.add("dve", lambda e: e.memset(r2(rr), 0.0), writes=["rr"])
        S.add("dve", lambda e: e.memset(r2(vnew), 0.0), writes=["vnew"])
        for blk in range(4):
            cs = slice(blk * 128, (blk + 1) * 128)
            S.add("dve", lambda e, blk=blk: e.tensor_tensor(rhs_d, bc(ident_f, [128, 8, 128], 1), bc(gcs[:, blk, h0:h0 + 8], [128, 8, 128], 2), op=ALU.mult),
                  reads=["gcs", "c_cmat"], writes=["rhs_d"])

            def mmrow(e):
                e.matmul(PS(4), lhsT=ones_f, rhs=r2(rhs_d)[:, 0:512], start=True, stop=True)
                return e.matmul(PS(5), lhsT=ones_f, rhs=r2(rhs_d)[:, 512:1024], start=True, stop=True)
            S.add("pe", mmrow, reads=["rhs_d", "c_cmat"], writes=PSK(4, 2))
            gcrow = PS(4, 2).rearrange("p (a b) -> p a b", a=8)

            def gram(e, cs=cs):
                ins = None
                for h8 in range(8):
                    o0 = h8 * 256
                    e.matmul(PS(0, 4)[:, o0:o0 + 128], lhsT=kT[:, h8, cs], rhs=kT[:, h8, cs], start=True, stop=True)
                    ins = e.matmul(PS(0, 4)[:, o0 + 128:o0 + 256], lhsT=kT[:, h8, cs], rhs=qT[:, h8, cs], start=True, stop=True)
                return ins
            S.add("pe", gram, reads=QK, writes=PSK(0, 4))
            G = PS(0, 4).rearrange("p (a b) -> p a b", a=8)
            S.add("act", lambda e: e.activation(EG, gcrow, AF.Exp), reads=PSK(4, 2), writes=["EG"])
            S.add("dve", lambda e, blk=blk: e.tensor_tensor(DT, gcrow, bc(gcs[:, blk, h0:h0 + 8], [128, 8, 128], 2), op=ALU.subtract),
                  reads=PSK(4, 2) + ["gcs"], writes=["DT"])
            S.add("dve", lambda e: e.tensor_tensor(DT, DT, bc(NEGm, [128, 8, 128], 1), op=ALU.add), reads=["DT", "c_cmat"], writes=["DT"])
            S.add("act", lambda e: e.activation(DT, DT, AF.Exp), reads=["DT"], writes=["DT"])
            S.add("dve", lambda e: e.tensor_tensor(AT, G[:, :, 128:256], DT, op=ALU.mult), reads=PSK(0, 4) + ["DT"], writes=["AT"])
            S.add("dve", lambda e: e.tensor_tensor(DT, DT, bc(STRm, [128, 8, 128], 1), op=ALU.mult), reads=["DT", "c_cmat"], writes=["DT"])
            S.add("dve", lambda e: e.tensor_tensor(Zf, G[:, :, 0:128], DT, op=ALU.mult), reads=PSK(0, 4) + ["DT"], writes=["Zf"])
            S.add("dve", lambda e, blk=blk: e.tensor_tensor(Zf, Zf, bc(nbeta[:, blk, h0:h0 + 8], [128, 8, 128], 2), op=ALU.mult),
                  reads=["Zf", "nbeta"], writes=["Zf"])
            S.add("dve", lambda e: e.tensor_tensor(Qf, Zf, bc(ident_f, [128, 8, 128], 1), op=ALU.add), reads=["Zf", "c_cmat"], writes=["Qf"])
            S.add("act", lambda e: e.activation(Xb[0], Zf, AF.Copy), reads=["Zf"], writes=[("Xb", 0)])
            S.add("act", lambda e: e.activation(Qb, Qf, AF.Copy), reads=["Qf"], writes=["Qb"])
            S.add("dve", lambda e, cs=cs: e.tensor_tensor(kdT, kT[:, :, cs], EG, op=ALU.mult), reads=QK + ["EG"], writes=["kdT"])
            S.add("dve", lambda e, cs=cs: e.tensor_tensor(qdT, qT[:, :, cs], EG, op=ALU.mult), reads=QK + ["EG"], writes=["qdT"])
            S.add("dve", lambda e, blk=blk: e.tensor_tensor(kdec, ktok[:, blk, :].rearrange("p (a b) -> p a b", a=8),
                                                            bc(ekl[:, blk, h0:h0 + 8], [128, 8, 128], 2), op=ALU.mult),
                  reads=["ktok", "ekl"], writes=["kdec"])
            pvb = PS(6).bitcast(BF16).rearrange("p (a b) -> p a b", a=8)

            def trz(e):
                ins = None
                for h8 in range(8):
                    ins = e.transpose(pvb[:, h8, :], Xb[0][:, h8, :], ident_b)
                return ins
            S.add("pe", trz, reads=[("Xb", 0), "c_cmatb0"], writes=PSK(6))
            S.add("dve", lambda e: e.tensor_copy(Wb[0], pvb), reads=PSK(6), writes=[("Wb", 0)])
            for n in range(5):
                xi, xo = n % 2, (n + 1) % 2
                if n < 4:
                    def mmx(e, xi=xi):
                        ins = None
                        for h8 in range(8):
                            ins = e.matmul(PS(0, 2)[:, h8 * 128:(h8 + 1) * 128], lhsT=Wb[xi][:, h8, :], rhs=Xb[xi][:, h8, :], start=True, stop=True)
                        return ins
                    S.add("pe", mmx, reads=[("Xb", xi), ("Wb", xi)], writes=PSK(0, 2))

                def mmw(e, xi=xi):
                    ins = None
                    for h8 in range(8):
                        ins = e.matmul(PS(2, 2)[:, h8 * 128:(h8 + 1) * 128], lhsT=Xb[xi][:, h8, :], rhs=Wb[xi][:, h8, :], start=True, stop=True)
                    return ins
                S.add("pe", mmw, reads=[("Xb", xi), ("Wb", xi)], writes=PSK(2, 2))
                if n < 4:
                    S.add("act", lambda e, xo=xo: e.activation(r2(Xb[xo]), PS(0, 2), AF.Copy), reads=PSK(0, 2), writes=[("Xb", xo)])
                S.add("dve", lambda e, xo=xo: e.tensor_copy(r2(Wb[xo]), PS(2, 2)), reads=PSK(2, 2), writes=[("Wb", xo)])

                def mmq(e, xo=xo):
                    ins = None
                    for h8 in range(8):
                        ins = e.matmul(PS(6, 2)[:, h8 * 128:(h8 + 1) * 128], lhsT=Wb[xo][:, h8, :], rhs=Qb[:, h8, :], start=True, stop=True)
                    return ins
                S.add("pe", mmq, reads=[("Wb", xo), "Qb"], writes=PSK(6, 2))
                S.add("dve", lambda e: e.tensor_tensor(r2(Qf), r2(Qf), PS(6, 2), op=ALU.add), reads=PSK(6, 2) + ["Qf"], writes=["Qf"])
                S.add("act", lambda e: e.activation(Qb, Qf, AF.Copy), reads=["Qf"], writes=["Qb"])
            if debug and t == 0 and hb == 0 and blk == 0:
                dma("sp", dbg["d_TT"], r2(Qb), ["Qb"], ["d_TT"], "st")
                dma("sp", dbg["d_AT"], r2(AT), ["AT"], ["d_AT"], "st")
            for c in range(2):
                ps_ = slice(64 * c, 64 * c + 64)

                def mmP(e):
                    ins = None
                    for h8 in range(8):
                        ins = e.matmul(PS(0, 2)[:, h8 * 128:(h8 + 1) * 128], lhsT=kdT[:, h8, :], rhs=Sbf[:, h0 + h8, :], start=True, stop=True)
                    return ins
                S.add("pe", mmP, reads=["kdT", "Sbf"], writes=PSK(0, 2))
                S.add("dve", lambda e, ps_=ps_, blk=blk: e.tensor_tensor(r2(rr)[ps_, :], vtok[ps_, blk, :], PS(0, 2)[ps_, :], op=ALU.subtract),
                      reads=PSK(0, 2) + ["vtok"], writes=["rr"])

                def mmV(e):
                    ins = None
                    for h8 in range(8):
                        ins = e.matmul(PS(2, 2)[:, h8 * 128:(h8 + 1) * 128], lhsT=Qb[:, h8, :], rhs=rr[:, h8, :], start=True, stop=True)
                    return ins
                S.add("pe", mmV, reads=["Qb", "rr"], writes=PSK(2, 2))
                S.add("dve", lambda e, ps_=ps_, blk=blk: e.tensor_tensor(vnew[ps_], PS(2, 2)[ps_, :].rearrange("p (a b) -> p a b", a=8),
                                                                         bc(beta[ps_, blk, h0:h0 + 8], [64, 8, 128], 2), op=ALU.mult),
                      reads=PSK(2, 2) + ["beta"], writes=["vnew"])

                def mmO(e):
                    ins = None
                    for h8 in range(8):
                        e.matmul(PS(4, 2)[:, h8 * 128:(h8 + 1) * 128], lhsT=qdT[:, h8, :], rhs=Sbf[:, h0 + h8, :], start=True, stop=False)
                        ins = e.matmul(PS(4, 2)[:, h8 * 128:(h8 + 1) * 128], lhsT=AT[:, h8, :], rhs=vnew[:, h8, :], start=False, stop=True)
                    return ins
                S.add("pe", mmO, reads=["qdT", "Sbf", "AT", "vnew"], writes=PSK(4, 2))
                S.add("act", lambda e, ps_=ps_: e.activation(r2(o_blk)[ps_, :], PS(4, 2)[ps_, :], AF.Copy), reads=PSK(4, 2), writes=["o_blk"])

                def mmS(e, ps_=ps_):
                    ins = None
                    for h8 in range(8):
                        ins = e.matmul(PS(6, 2)[:, h8 * 128:(h8 + 1) * 128], lhsT=kdec[ps_, h8, :], rhs=vnew[ps_, h8, :], start=True, stop=True)
                    return ins
                S.add("pe", mmS, reads=["kdec", "vnew"], writes=PSK(6, 2))
                lastc = 64 * c + 63
                S.add("dve", lambda e, lastc=lastc: e.tensor_tensor(Sst[:, h0:h0 + 8, :], Sst[:, h0:h0 + 8, :], bc(EG[:, :, lastc], [128, 8, 128], 2), op=ALU.mult),
                      reads=["Sst", "EG"], writes=["Sst"])
                S.add("dve", lambda e: e.tensor_tensor(Sst[:, h0:h0 + 8, :], Sst[:, h0:h0 + 8, :], PS(6, 2).rearrange("p (a b) -> p a b", a=8), op=ALU.add),
                      reads=["Sst"] + PSK(6, 2), writes=["Sst"])
                S.add("act", lambda e: e.activation(Sbf[:, h0:h0 + 8, :], Sst[:, h0:h0 + 8, :], AF.Copy), reads=["Sst"], writes=["Sbf"])
            if debug and t == 0 and hb == 0 and blk == 0:
                dma("sp", dbg["d_oblk"], r2(o_blk), ["o_blk"], ["d_oblk"], "st")
            S.add("dve", lambda e: e.tensor_tensor(tmpf, o_blk, o_blk, op=ALU.mult), reads=["o_blk"], writes=["tmpf"])
            S.add("dve", lambda e: e.reduce_sum(ssh[:, 0:8], tmpf, axis=AX.X), reads=["tmpf"], writes=["ssh"])
            S.add("act", lambda e: e.activation(ssh[:, 0:8], ssh[:, 0:8], AF.Sqrt, scale=1.0 / 128, bias=EPS), reads=["ssh"], writes=["ssh"])
            S.add("dve", lambda e: e.reciprocal(ssh[:, 0:8], ssh[:, 0:8]), reads=["ssh"], writes=["ssh"])
            S.add("dve", lambda e: e.tensor_tensor(tmpf, o_blk, bc(ssh[:, 0:8], [128, 8, 128], 2), op=ALU.mult), reads=["o_blk", "ssh"], writes=["tmpf"])
            S.add("dve", lambda e: e.tensor_tensor(tmpf, tmpf, bc(normw_b, [128, 8, 128], 1), op=ALU.mult), reads=["tmpf", "c_normw"], writes=["tmpf"])
            S.add("dve", lambda e, blk=blk: e.tensor_tensor(on_tok, tmpf, zstok[:, blk, :].rearrange("p (a b) -> p a b", a=8), op=ALU.mult),
                  reads=["tmpf", "zstok"], writes=["on_tok"])
            pvo = PS(0).bitcast(BF16).rearrange("p (a b) -> p a b", a=8)

            def tro(e):
                ins = None
                for h8 in range(8):
                    ins = e.transpose(pvo[:, h8, :], on_tok[:, h8, :], ident_b)
                return ins
            S.add("pe", tro, reads=["on_tok", "c_cmatb0"], writes=PSK(0))
            S.add("act", lambda e, cs=cs: e.activation(onT[:, h0:h0 + 8, cs], pvo, AF.Copy), reads=PSK(0), writes=["onT"])

    def phaseC(wsrc, nk, actT_, akey, gpost_idx):
        dma("sp", gain_bc, gpost_d[gpost_idx].partition_broadcast(128), [], ["gain_bc"], "ld")
        cols = [(c * 128, 128, c) for c in range(KC)]
        slot = [0]

        def consume(tag, b, cn):
            c = tag
            s = slot[0] % 3
            bank = slot[0] % 2
            slot[0] += 1
            S.add("act", lambda e, s=s, b=b: e.activation(mT[:, s, :], PS(b), AF.Copy), reads=PSK(b), writes=[("mT", s)])

            def evm(pv, pk, c=c):
                S.add("dve", lambda e: e.tensor_copy(mtok[:, :, c * 128:(c + 1) * 128], pv), reads=pk, writes=[("mtok", b_) for b_ in range(4)])
            def st2():
                pv, pk = transposes_pe(mT[:, s, :], ("mT", s), F32, bank)
                evm(pv, pk)
                return None
            return st2
        proj_chunks(wsrc, nk, actT_, akey, cols, consume, lag=1)
        if stop <= 4.95:
            raise StopBuild()
        for blk in range(4):
            S.add("act", lambda e, blk=blk: e.activation(junk_a, mtok[:, blk, :], AF.Square, accum_out=ss[:, blk:blk + 1]),
                  reads=[("mtok", blk)], writes=[("ss", blk)])
            S.add("act", lambda e, blk=blk: e.activation(rstd[:, blk:blk + 1], ss[:, blk:blk + 1], AF.Sqrt, scale=1.0 / D, bias=EPS),
                  reads=[("ss", blk)], writes=[("rstd", blk)])
            S.add("dve", lambda e, blk=blk: e.reciprocal(rstd[:, blk:blk + 1], rstd[:, blk:blk + 1]),
                  reads=[("rstd", blk)], writes=[("rstd", blk)])
            S.add("dve", lambda e, blk=blk: e.scalar_tensor_tensor(mtok[:, blk, :], mtok[:, blk, :], rstd[:, blk:blk + 1], gain_bc,
                                                                   op0=ALU.mult, op1=ALU.mult),
                  reads=[("mtok", blk), ("rstd", blk), "gain_bc"], writes=[("mtok", blk)])
            S.add("dve", lambda e, blk=blk: e.tensor_tensor(h_tok[:, blk, :], h_tok[:, blk, :], mtok[:, blk, :], op=ALU.add),
                  reads=[("mtok", blk), ("h", blk)], writes=[("h", blk)])


    def ffn(layer, halo_only=False):
        rmsnorm_T(ALLH, 1 if layer == 0 else 4, hT, "hT", R_hnb2)
        cols = []
        for g in range(FC // 2):
            cols.append((g * 256, 128, ("g", 2 * g)))
            cols.append((g * 256 + 128, 128, ("g", 2 * g + 1)))
            if not halo_only:
                cols.append((DFF + g * 256, 128, ("u", 2 * g)))
                cols.append((DFF + g * 256 + 128, 128, ("u", 2 * g + 1)))
        slot = [0]
        pend = {}

        def consume(tag, b, cn):
            typ, c = tag
            if typ == "g":
                s = slot[0] % 3
                slot[0] += 1
                S.add("dve", lambda e, s=s, c=c: e.tensor_copy(gpre[:, s, 0:2], halo_g[:, c, :]), reads=[("halog", c)], writes=[("gpre", s)])
                S.add("act", lambda e, s=s, b=b: e.activation(gpre[:, s, 2:514], PS(b), AF.Copy), reads=PSK(b), writes=[("gpre", s)])
                pend[c] = s

                def st_conv():
                    S.add("dve", lambda e: e.tensor_copy(halo_g[:, c, :], gpre[:, s, 512:514]), reads=[("gpre", s)], writes=[("halog", c)])
                    if halo_only:
                        return None
                    S.add("dve", lambda e: e.tensor_scalar(gacc[:, s, :], gpre[:, s, 0:512], fconvw[:, layer, c, 0:1], None, op0=ALU.mult),
                          reads=[("gpre", s), "c_fconvw"], writes=[("gacc", s)])
                    for j in range(1, 3):
                        S.add("dve", lambda e, j=j: e.scalar_tensor_tensor(gacc[:, s, :], gpre[:, s, j:j + 512], fconvw[:, layer, c, j:j + 1], gacc[:, s, :], op0=ALU.mult, op1=ALU.add),
                              reads=[("gpre", s), "c_fconvw", ("gacc", s)], writes=[("gacc", s)])

                    def st_silu():
                        S.add("act", lambda e: e.activation(gacc[:, s, :], gacc[:, s, :], AF.Silu, bias=fconvb[:, layer, c:c + 1]),
                              reads=[("gacc", s), "c_fconvb"], writes=[("gacc", s)])
                        return None
                    return st_silu
                return st_conv
            else:
                s = pend.pop(c)

                def st_w1():
                    def st_mul():
                        S.add("dve", lambda e: e.tensor_tensor(actT[:, c, :], gacc[:, s, :], PS(b), op=ALU.mult),
                              reads=[("gacc", s)] + PSK(b), writes=["actT"])
                        return None
                    return st_mul
                return st_w1
        proj_chunks(w_up[layer], KC, hT, HTK, cols, consume, banks=(0, 1, 2, 3, 4, 5, 6, 7), lag=1)
        S.barrier()
        if halo_only:
            return
        phaseC(w_dn[layer], FC, actT, ["actT"], 1 if layer == 0 else 3)

    def kv_tile(t):
        S.barrier()
        rmsnorm_T(ALLH, 2, hT, "hT", R_hnb2)
        cols = [(c * 128, 128, c) for c in range(32)]
        slot = [0]

        def consume(tag, b, cn):
            c = tag
            s = slot[0] % 3
            bank = slot[0] % 2
            slot[0] += 1
            S.add("act", lambda e, s=s, b=b: e.activation(kvT[:, s, :], PS(b), AF.Copy), reads=PSK(b), writes=[("kvT", s)])
            if c < 16:
                dma("sp", KT_d[c, :, t * TT:(t + 1) * TT], kvT[:, s, :], [("kvT", s)], [("KT_d", c)], "st")
                return None
            cv = c - 16

            def st2():
                def evv(pv, pk, cv=cv):
                    S.add("dve", lambda e: e.tensor_copy(vtok2[:, :, cv * 128:(cv + 1) * 128], pv), reads=pk, writes=["vtok2"])
                transposes_to_tok(kvT[:, s, :], ("kvT", s), evv, None, BF16, bank=bank)
                return None
            return st2
        proj_chunks(w_kv, KC, hT, HTK, cols, consume)
        for blk in range(4):
            dma("sp", V_d[t * TT + blk * 128:t * TT + (blk + 1) * 128, :], vtok2[:, blk, :], ["vtok2"], ["V_d"], "st")

    R.reset()
    KTs = R.alloc([2, SEXT], BF16)
    Vs = R.alloc([NKB, 258], BF16)
    qTs = R.alloc([2, TT], BF16)
    PTs = R.alloc([3, TT], BF16)
    om = [R.alloc([4, 256], F32), R.alloc([4, 256], F32)]
    ocomb = R.alloc([4, 256], F32)
    otmp = R.alloc([4, 256], F32)
    attn_tok = R.alloc([4, D], BF16)
    R_hnb3 = R.alloc([2, D], BF16)
    SCALE = 128 ** -0.5
    lamv = smallf[:, 480:484]
    den = smallf[:, 488:496]
    ss4 = smallf[:, 496:500]

    def l1_setup():
        for i in range(2):
            S.add("dve", lambda e, i=i: e.tensor_tensor(lq_b[:, 2 * i, :], lq_b[:, 2 * i, :], lq_b[:, 2 * i + 1, :], op=ALU.mult),
                  reads=[("c_lq", 2 * i), ("c_lq", 2 * i + 1)], writes=[("c_lq", 2 * i)])
            S.add("dve", lambda e, i=i: e.reduce_sum(lamv[:, i:i + 1], lq_b[:, 2 * i, :], axis=AX.X), reads=[("c_lq", 2 * i)], writes=[("lam", i)])
            S.add("act", lambda e, i=i: e.activation(lamv[:, i:i + 1], lamv[:, i:i + 1], AF.Exp), reads=[("lam", i)], writes=[("lam", i)])
        S.add("dve", lambda e: e.tensor_tensor(lamv[:, 2:3], lamv[:, 1:2], lamv[:, 0:1], op=ALU.subtract), reads=[("lam", 0), ("lam", 1)], writes=["nlam"])
        S.add("dve", lambda e: e.tensor_scalar(lamv[:, 2:3], lamv[:, 2:3], -LAMBDA_INIT, None, op0=ALU.add), reads=["nlam"], writes=["nlam"])
        S.add("dve", lambda e: e.tensor_scalar(subln_b, subln_b, 1.0 - LAMBDA_INIT, None, op0=ALU.mult), reads=["c_subln"], writes=["c_subln"])

    def l1_tile(t, halo=False):
        S.barrier()
        nkb = 4 * (t + 1)
        nkeys = nkb * 128
        for blk in range(4):
            r0 = (t - OWN0 + HB) * TT + blk * 128
            dma("sp", h_tok[:, blk, :], h1_d[r0:r0 + 128, :], ["h1_d"], [("h", blk)], "ld")
        rmsnorm_T(ALLH, 3, hT, "hT", R_hnb3)
        S.add("dve", lambda e: e.memset(Vs[:, :, 256:258], 1.0), writes=["Vs"])
        for hd in range(8):
            for m in range(2):
                dma("sp", KTs[:, m, 0:nkeys], KT_d[2 * hd + m, :, 0:nkeys], [("KT_d", 2 * hd + m)], ["KTs"], "ld")
            dma("sp", Vs[:, 0:nkb, 0:256], V_d[0:nkeys, hd * 256:(hd + 1) * 256].rearrange("(kb p) e -> p kb e", p=128), ["V_d"], ["Vs"], "ld")

            def consq(tag, b, cn):
                S.add("act", lambda e, tag=tag, b=b: e.activation(qTs[:, tag, :], PS(b), AF.Copy), reads=PSK(b), writes=[("qTs", tag)])
            proj_chunks(w_q, KC, hT, HTK, [(hd * 256, 128, 0), (hd * 256 + 128, 128, 1)], consq, banks=(2, 3))
            sctr = 0
            for m in range(2):
                prev = None
                for kb in range(nkb + 1):
                    cur = None
                    if kb < nkb:
                        q0 = max(0, kb - 4 * t)
                        qs = slice(q0 * 128, TT)
                        sb_ = 2 + (sctr % 2)
                        sl = sctr % 3
                        sctr += 1
                        S.add("pe", lambda e, m=m, kb=kb, qs=qs, sb_=sb_: e.matmul(PS(sb_)[:, qs], lhsT=KTs[:, m, kb * 128:(kb + 1) * 128], rhs=qTs[:, m, qs], start=True, stop=True),
                              reads=["KTs", ("qTs", m)], writes=PSK(sb_))
                        S.add("act", lambda e, kb=kb, qs=qs, sb_=sb_, sl=sl: e.activation(PTs[:, sl, qs], PS(sb_)[:, qs], AF.Exp, scale=SCALE, bias=maskb[:, kb:kb + 1]),
                              reads=PSK(sb_) + ["c_maskb"], writes=[("PT", sl)])
                        if kb >= 4 * t:
                            S.add("dve", lambda e, q0=q0, sl=sl: e.memset(PTs[64:128, sl, q0 * 128:q0 * 128 + 64], 0.0), reads=[("PT", sl)], writes=[("PT", sl)])
                        cur = (kb, q0, sl)
                    if prev is not None:
                        pkb, pq0, psl = prev
                        for qb in range(pq0, 4):
                            def av(e, pkb=pkb, qb=qb, psl=psl, t=t):
                                e.matmul(PS(4 + qb)[:, 0:256], lhsT=PTs[:, psl, qb * 128:(qb + 1) * 128], rhs=Vs[:, pkb, 0:256],
                                         start=(pkb == 0), stop=(pkb == 4 * t + qb))
                                return e.matmul(PS(4 + qb)[:, 256:257], lhsT=PTs[:, psl, qb * 128:(qb + 1) * 128], rhs=ones_b[:, 0:1],
                                                start=False, stop=(pkb == 4 * t + qb), skip_group_check=True)
                            S.add("pe", av, reads=[("PT", psl), "Vs", "c_cmatb1"], writes=PSK(4 + qb))
                    prev = cur
                for qb in range(4):
                    S.add("dve", lambda e, qb=qb, m=m: e.tensor_scalar(den[:, 4 * m + qb:4 * m + qb + 1], PS(4 + qb)[:, 256:257], 1e-30, None, op0=ALU.max),
                          reads=PSK(4 + qb), writes=[("den", m, qb)])
                    S.add("dve", lambda e, qb=qb, m=m: e.reciprocal(den[:, 4 * m + qb:4 * m + qb + 1], den[:, 4 * m + qb:4 * m + qb + 1]),
                          reads=[("den", m, qb)], writes=[("den", m, qb)])
                    S.add("act", lambda e, qb=qb, m=m: e.activation(om[m][:, qb, :], PS(4 + qb)[:, 0:256], AF.Copy, scale=den[:, 4 * m + qb:4 * m + qb + 1]),
                          reads=PSK(4 + qb) + [("den", m, qb)], writes=[("om", m)])
            S.add("dve", lambda e: e.scalar_tensor_tensor(ocomb, om[1], lamv[:, 2:3], om[0], op0=ALU.mult, op1=ALU.add),
                  reads=[("om", 0), ("om", 1), "nlam"], writes=["ocomb"])
            S.add("dve", lambda e: e.tensor_tensor(otmp, ocomb, ocomb, op=ALU.mult), reads=["ocomb"], writes=["otmp"])
            S.add("dve", lambda e: e.reduce_sum(ss4, otmp, axis=AX.X), reads=["otmp"], writes=["ss4"])
            S.add("act", lambda e: e.activation(ss4, ss4, AF.Sqrt, scale=1.0 / 256, bias=EPS), reads=["ss4"], writes=["ss4"])
            S.add("dve", lambda e: e.reciprocal(ss4, ss4), reads=["ss4"], writes=["ss4"])
            S.add("dve", lambda e: e.tensor_tensor(otmp, ocomb, bc(ss4, [128, 4, 256], 2), op=ALU.mult), reads=["ocomb", "ss4"], writes=["otmp"])
            S.add("dve", lambda e, hd=hd: e.tensor_tensor(attn_tok[:, :, hd * 256:(hd + 1) * 256], otmp, bc(subln_b, [128, 4, 256], 1), op=ALU.mult),
                  reads=["otmp", "c_subln"], writes=["attn_tok"])
        for blk in range(4):
            for half in range(2):
                pvo = PS(half).bitcast(BF16).rearrange("p (a b) -> p a b", a=8)

                def tra(e, blk=blk, half=half, pvo=pvo):
                    ins = None
                    for j in range(8):
                        kc = half * 8 + j
                        ins = e.transpose(pvo[:, j, :], attn_tok[:, blk, kc * 128:(kc + 1) * 128], ident_b)
                    return ins
                S.add("pe", tra, reads=["attn_tok", "c_cmatb0"], writes=PSK(half))
                S.add("act", lambda e, blk=blk, half=half, pvo=pvo: e.activation(onT[:, half * 8:half * 8 + 8, blk * 128:(blk + 1) * 128], pvo, AF.Copy),
                      reads=PSK(half), writes=["onT"])
        S.barrier()
        phaseC(w_o, KC, onT, ["onT"], 2)
        ffn(1, halo_only=halo)
        if not halo:
            for blk in range(4):
                r0 = (t - OWN0) * TT + blk * 128
                dma("sp", out[r0:r0 + 128, :], h_tok[:, blk, :], [("h", blk)], ["out"], "st")

    for t in range(NT):
        try:
            layer0_tile(t)
        except StopBuild:
            break
        if do_l1:
            kv_tile(t)
        if do_l1 and t >= OWN0 - HB:
            for blk in range(4):
                r0 = (t - OWN0 + HB) * TT + blk * 128
                dma("sp", h1_d[r0:r0 + 128, :], h_tok[:, blk, :], [("h", blk)], ["h1_d"], "st")
        if t >= OWN0:
            if do_l1:
                pass
            else:
                for blk in range(4):
                    r0 = (t - OWN0) * TT + blk * 128
                    dma("sp", out[r0:r0 + 128, :], h_tok[:, blk, :], [("h", blk)], ["out"], "st")
        S.barrier()

    if do_l1 and stop > 50:
        l1_setup()
        if HB:
            l1_tile(OWN0 - 1, halo=True)
        else:
            S.add("dve", lambda e: e.memset(halo_g, 0.0), reads=[("halog", c) for c in range(FC)], writes=[("halog", c) for c in range(FC)])
        for t in range(OWN0, NT):
            l1_tile(t)
    S.barrier()
    S.add("sp", lambda e: e.dma_start(out=h1_d[0:1, 0:64], in_=x[0:1, 0:64]), reads=[], writes=["fin"], dma="st")
    S.barrier()
    S.add("act", lambda e: e.activation(ss, ss, AF.Copy), reads=["fin"], writes=["fin2"])
    S.emit(nc, es)
    es.close()
    return nc, S


def make_consts():
    ident = np.eye(128, dtype=np.float32)
    j = np.arange(128)[:, None]
    i = np.arange(128)[None, :]
    same = (j // 64) == (i // 64)
    NEGm = np.where(same & (i >= j), 0.0, NEG).astype(np.float32)
    STRm = (same & (i > j)).astype(np.float32)
    TRI_T = (same & (j <= i)).astype(np.float32)
    BLK = same.astype(np.float32)
    ones = np.ones((128, 128), np.float32)
    return np.stack([ident, NEGm, STRm, TRI_T, BLK, ones], 0)


def host_inputs(inputs, NT=8, OWN0=4):
    f = lambda a: np.ascontiguousarray(np.asarray(a, dtype=np.float32))
    x = f(inputs["x"])
    B = x.shape[0]
    fm = lambda v: np.ascontiguousarray(np.asarray(v, np.float32).reshape(16, 128).T)
    gp = np.concatenate([fm(inputs["norm_mix_pre"][0]), fm(inputs["norm_ffn_pre"][0]), fm(inputs["kv_norm_w"]),
                         fm(inputs["norm_mix_pre"][1]), fm(inputs["norm_ffn_pre"][1])], axis=1)
    gpost = np.stack([f(inputs["norm_mix_post"][0]), f(inputs["norm_ffn_post"][0]), f(inputs["norm_mix_post"][1]), f(inputs["norm_ffn_post"][1])], 0)
    cw = f(inputs["gdn_conv_w"][0])
    convw = np.ascontiguousarray(cw.T.reshape(48, 128, 4).transpose(1, 0, 2).reshape(128, 48 * 4))
    fw = f(inputs["ffn_conv_w"])
    fconvw = np.ascontiguousarray(fw.transpose(0, 2, 1).reshape(2, FC, 128, 3).transpose(2, 0, 1, 3).reshape(128, 2 * FC * 3))
    fb = f(inputs["ffn_conv_b"])
    fconvb = np.ascontiguousarray(fb.reshape(2, FC, 128).transpose(2, 0, 1).reshape(128, 2 * FC))
    lq = np.stack([f(inputs["diff_lq1"][0]), f(inputs["diff_lk1"][0]), f(inputs["diff_lq2"][0]), f(inputs["diff_lk2"][0])], 0)
    common = {
        "w_in": f(inputs["gdn_w_in"][0]), "w_out": f(inputs["gdn_w_out"][0]), "w_kv": f(inputs["w_kv"]),
        "w_q": f(inputs["diff_w_q"][0]), "w_o": f(inputs["diff_w_o"][0]),
        "w_up0": f(inputs["ffn_w_up"][0]), "w_up1": f(inputs["ffn_w_up"][1]),
        "w_dn0": f(inputs["ffn_w_down"][0]), "w_dn1": f(inputs["ffn_w_down"][1]),
        "gp": gp, "gpost": gpost, "convw": convw, "fconvw": fconvw, "fconvb": fconvb,
        "alog": f(inputs["gdn_a_log"][0]), "dtb": f(inputs["gdn_dt_bias"][0]), "normw": f(inputs["gdn_norm_w"][0]),
        "subln": f(inputs["diff_subln_w"][0]), "lq": lq, "cmat": make_consts(),
    }
    SEXT = NT * TT
    SOWN = (NT - OWN0) * TT
    maps = []
    for c in range(8):
        b, half = c // 2, c % 2
        xe = np.zeros((SEXT, D), np.float32)
        mask = np.zeros((SEXT,), np.float32)
        if half == 1:
            xe[:] = x[b, 0:SEXT]
        else:
            xe[SEXT - SOWN:] = x[b, 0:SOWN]
            mask[:SEXT - SOWN] = NEG
        m = dict(common)
        m["x"] = xe
        mk = np.zeros((128, 32), np.float32)
        mk[:, :SEXT // 128] = mask.reshape(SEXT // 128, 128).T
        m["maskb"] = mk
        maps.append(m)
    return maps


_CACHE = {}


def kernel(**inputs):
    NT, OWN0 = 8, 4
    if "nc" not in _CACHE:
        _CACHE["nc"] = build_program(NT, OWN0, debug=False, do_l1=True)[0]
    nc = _CACHE["nc"]
    maps = host_inputs(inputs, NT, OWN0)
    res = run_bass_kernel_spmd(nc, maps, core_ids=list(range(8)))
    x = np.asarray(inputs["x"])
    B, Sq, _ = x.shape
    outp = np.zeros((B, Sq, D), np.float32)
    for c in range(8):
        b, half = c // 2, c % 2
        outp[b, half * 2048:(half + 1) * 2048] = res.results[c]["out"]
    return outp
```
